# Optimizing a Trainium2 kernel written in Bass

```python
import jax, jax.numpy as jnp
from jax import lax
import numpy as np

D_MODEL = 1024
BATCH = 16
SEQ = 256
DEPTH = 4
DEC_BATCH = 8
DEC_SEQ = 1024
PAST_LEN = 256

GRID_W = 64
A_W = D_MODEL // 2
NA_HEADS = 8
NA_DH = (D_MODEL // 2) // NA_HEADS
NA_W = NA_HEADS * NA_DH
KH_MAX = 8
KW = 16
QB = 16
KB = QB + KW
POOL_W = D_MODEL // 2
POOL_WINDOWS = (2, 4, 8, 16)
N_POOL_GROUPS = 4
POOL_G = POOL_W // N_POOL_GROUPS
MLA_HEADS = 8
NOPE = 64
ROPE = 32
QK_DIM = NOPE + ROPE
V_DIM = 64
Q_LORA = 384
KV_LORA = 256
ROPE_BASE = 10000.0
MLP_HIDDEN = 4 * D_MODEL
N_EVEN = (DEPTH + 1) // 2
N_ODD = DEPTH // 2
Q_BLOCK = 128
DENSE_KEY_LIMIT = 1024
NEG_INF = -1e30
EPS = 1e-6

kernel_name = "hybrid_dit_prefix_ctx_step"


def rms_norm(x, g):
    xf = x.astype(jnp.float32)
    y = xf * lax.rsqrt(jnp.mean(xf * xf, axis=-1, keepdims=True) + EPS)
    return (y * g.astype(jnp.float32)).astype(x.dtype)


def ada_mod(s, w, b):
    m = (s @ w + b)[:, None, :]
    return jnp.split(m, 6, axis=-1)


def modulate(x, g, shift, scale):
    return rms_norm(x, g) * (1 + scale) + shift


def sq_relu_mlp(h, w1, w2):
    a = jax.nn.relu(h @ w1)
    return (a * a) @ w2


def short_conv(u, w):
    up = jnp.pad(u, ((0, 0), (1, 1), (0, 0)))
    return up[:, :-2] * w[0] + up[:, 1:-1] * w[1] + up[:, 2:] * w[2]


def pool_mixer(u, w_groups, scale):
    B, L, _ = u.shape
    ug = u.reshape(B, L, N_POOL_GROUPS, POOL_G)
    cs = jnp.pad(jnp.cumsum(ug.astype(jnp.float32), axis=1), ((0, 0), (1, 0), (0, 0), (0, 0)))
    t = jnp.arange(L)
    means = []
    for g, w in enumerate(POOL_WINDOWS):
        lo = jnp.clip(t - w // 2, 0, L)
        hi = jnp.clip(t - w // 2 + w, 0, L)
        s = cs[:, hi, g] - cs[:, lo, g]
        means.append(s / (hi - lo).astype(jnp.float32)[None, :, None])
    pooled = jnp.stack(means, axis=2).astype(u.dtype)
    y = jnp.einsum('blgc,gcd->blgd', pooled - ug, w_groups).reshape(B, L, POOL_W)
    return y * scale


def axial_rope(n):
    t = jnp.arange(n)
    row = (t // GRID_W).astype(jnp.float32)
    col = (t % GRID_W).astype(jnp.float32)
    axis_dim = ROPE // 2
    inv = 1.0 / (ROPE_BASE ** (jnp.arange(0, axis_dim, 2, dtype=jnp.float32) / axis_dim))
    ang = jnp.concatenate([row[:, None] * inv, col[:, None] * inv], axis=-1)
    return jnp.cos(ang), jnp.sin(ang)


def apply_rope(x, cos, sin):
    x1, x2 = x[..., :ROPE // 2], x[..., ROPE // 2:]
    c = cos[None, :, None, :].astype(x.dtype)
    s = sin[None, :, None, :].astype(x.dtype)
    return jnp.concatenate([x1 * c - x2 * s, x2 * c + x1 * s], axis=-1)


def rope_heads(x, cos, sin):
    return jnp.concatenate([x[..., :NOPE], apply_rope(x[..., NOPE:], cos, sin)], axis=-1)


def dense_attn(q, k, v):
    B, Lq, H, d = q.shape
    Lk = k.shape[1]
    scale = d ** -0.5

    def attend(qb):
        s = jnp.einsum('bqhd,bkhd->bhqk', qb, k).astype(jnp.float32) * scale
        p = jax.nn.softmax(s, axis=-1).astype(v.dtype)
        return jnp.einsum('bhqk,bkhd->bqhd', p, v)

    if Lk < DENSE_KEY_LIMIT:
        return attend(q)
    nb = Lq // Q_BLOCK
    qb = jnp.moveaxis(q.reshape(B, nb, Q_BLOCK, H, d), 1, 0)
    out = lax.map(attend, qb)
    return jnp.moveaxis(out, 0, 1).reshape(B, Lq, H, v.shape[-1])


def na_latent(q, k, v, k_ctx, v_ctx, rpb):
    B, N, H, dh = q.shape
    R = N // GRID_W
    KH = min(KH_MAX, R)
    NCB = GRID_W // QB
    scale = dh ** -0.5
    qg = q.reshape(B, R, NCB, QB, H, dh)
    kg = k.reshape(B, R, GRID_W, H, dh)
    vg = v.reshape(B, R, GRID_W, H, dh)
    r = jnp.arange(R)
    rows_idx = jnp.clip(r - KH // 2, 0, R - KH)[:, None] + jnp.arange(KH)
    col_start = jnp.clip(jnp.arange(GRID_W) - KW // 2, 0, GRID_W - KW)
    j = jnp.arange(NCB)
    cols_idx = jnp.clip(j * QB - KW // 2, 0, GRID_W - KB)[:, None] + jnp.arange(KB)
    kb = kg[:, rows_idx][:, :, :, cols_idx]
    vb = vg[:, rows_idx][:, :, :, cols_idx]
    s_loc = jnp.einsum('brcqhd,brkcjhd->bhrcqkj', qg, kb).astype(jnp.float32) * scale
    qcol = (j * QB)[:, None] + jnp.arange(QB)
    qcs = col_start[qcol]
    keycol = cols_idx[:, None, :]
    valid = (keycol >= qcs[:, :, None]) & (keycol < qcs[:, :, None] + KW)
    dr = rows_idx - r[:, None] + KH_MAX - 1
    dc = jnp.clip(keycol - qcol[:, :, None] + KW - 1, 0, 2 * KW - 2)
    bias = rpb[:, dr[:, None, None, :, None], dc[None, :, :, None, :]]
    s_loc = s_loc + bias[None].astype(jnp.float32)
    s_loc = jnp.where(valid[:, :, None, :], s_loc, NEG_INF).reshape(B, H, R, NCB, QB, KH * KB)
    s_ctx = jnp.einsum('brcqhd,bmhd->bhrcqm', qg, k_ctx).astype(jnp.float32) * scale
    p = jax.nn.softmax(jnp.concatenate([s_loc, s_ctx], axis=-1), axis=-1)
    p_loc = p[..., :KH * KB].reshape(B, H, R, NCB, QB, KH, KB).astype(v.dtype)
    p_ctx = p[..., KH * KB:].astype(v.dtype)
    out = (jnp.einsum('bhrcqkj,brkcjhd->brcqhd', p_loc, vb)
           + jnp.einsum('bhrcqm,bmhd->brcqhd', p_ctx, v_ctx))
    return out.reshape(B, N, H * dh)


def even_proj(h, w_in, conv_w, gq, gk):
    B, L, _ = h.shape
    bg, cg, xa, q, k, v = jnp.split(h @ w_in, 6, axis=-1)
    a = bg * short_conv(cg * xa, conv_w)
    q = rms_norm(q.reshape(B, L, NA_HEADS, NA_DH), gq)
    k = rms_norm(k.reshape(B, L, NA_HEADS, NA_DH), gk)
    v = v.reshape(B, L, NA_HEADS, NA_DH)
    return a, q, k, v


def odd_proj(h, w_in, pool_w, pool_scale, q_a_norm, w_q_b, kv_a_norm, gq):
    B, L, _ = h.shape
    u, q_lat, kv_lat, kpe = jnp.split(
        h @ w_in, [POOL_W, POOL_W + Q_LORA, POOL_W + Q_LORA + KV_LORA], axis=-1)
    p = pool_mixer(u, pool_w, pool_scale)
    q = (rms_norm(q_lat, q_a_norm) @ w_q_b).reshape(B, L, MLA_HEADS, QK_DIM)
    q = rms_norm(q, gq)
    ckv = rms_norm(kv_lat, kv_a_norm)
    return p, q, ckv, kpe


def mla_kv(ckv, kpe, w_kv_b, gk):
    B, L, _ = ckv.shape
    kv = (ckv @ w_kv_b).reshape(B, L, MLA_HEADS, NOPE + V_DIM)
    k_nope, v = kv[..., :NOPE], kv[..., NOPE:]
    k = jnp.concatenate(
        [k_nope, jnp.broadcast_to(kpe[:, :, None, :], (B, L, MLA_HEADS, ROPE))], axis=-1)
    return rms_norm(k, gk), v


def setup_inputs(seed: int = 0) -> dict:
    key = jax.random.key(seed)
    ks = jax.random.split(key, 32)
    f32 = jnp.float32

    def nrm(k, shape, scale=1.0):
        return jax.random.normal(k, shape, f32) * scale

    def gain(k, shape):
        return 1.0 + 0.1 * jax.random.normal(k, shape, f32)

    D = D_MODEL
    even_in = 3 * A_W + 3 * NA_W
    odd_in = POOL_W + Q_LORA + KV_LORA + ROPE
    return {
        "x_prompt": nrm(ks[0], (BATCH, SEQ, D)),
        "x_sample": nrm(ks[1], (DEC_BATCH, DEC_SEQ, D)),
        "cache_na_k": nrm(ks[2], (DEC_BATCH, N_EVEN, PAST_LEN, NA_HEADS, NA_DH)),
        "cache_na_v": nrm(ks[3], (DEC_BATCH, N_EVEN, PAST_LEN, NA_HEADS, NA_DH)),
        "cache_mla_ckv": nrm(ks[4], (DEC_BATCH, N_ODD, PAST_LEN, KV_LORA)),
        "cache_mla_kpe": nrm(ks[5], (DEC_BATCH, N_ODD, PAST_LEN, ROPE)),
        "c": nrm(ks[6], (DEC_BATCH, D)),
        "c_ctx": nrm(ks[7], (D,)),
        "ada_w": nrm(ks[8], (DEPTH, D, 6 * D), 0.5 * D ** -0.5),
        "ada_b": nrm(ks[9], (DEPTH, 6 * D), 0.02),
        "norm1_g": gain(ks[10], (DEPTH, D)),
        "norm2_g": gain(ks[11], (DEPTH, D)),
        "mlp_w1": nrm(ks[12], (DEPTH, D, MLP_HIDDEN), D ** -0.5),
        "mlp_w2": nrm(ks[13], (DEPTH, MLP_HIDDEN, D), MLP_HIDDEN ** -0.5),
        "even_w_in": nrm(ks[14], (N_EVEN, D, even_in), D ** -0.5),
        "even_conv_w": nrm(ks[15], (N_EVEN, 3, A_W), 3 ** -0.5),
        "na_q_norm": gain(ks[16], (N_EVEN, NA_DH)),
        "na_k_norm": gain(ks[17], (N_EVEN, NA_DH)),
        "na_rpb": nrm(ks[18], (N_EVEN, NA_HEADS, 2 * KH_MAX - 1, 2 * KW - 1), 0.1),
        "even_w_out": nrm(ks[19], (N_EVEN, A_W + NA_W, D), (A_W + NA_W) ** -0.5),
        "odd_w_in": nrm(ks[20], (N_ODD, D, odd_in), D ** -0.5),
        "pool_w": nrm(ks[21], (N_ODD, N_POOL_GROUPS, POOL_G, POOL_G), POOL_G ** -0.5),
        "pool_scale": gain(ks[22], (N_ODD, POOL_W)),
        "q_a_norm": gain(ks[23], (N_ODD, Q_LORA)),
        "w_q_b": nrm(ks[24], (N_ODD, Q_LORA, MLA_HEADS * QK_DIM), Q_LORA ** -0.5),
        "kv_a_norm": gain(ks[25], (N_ODD, KV_LORA)),
        "w_kv_b": nrm(ks[26], (N_ODD, KV_LORA, MLA_HEADS * (NOPE + V_DIM)), KV_LORA ** -0.5),
        "mla_q_norm": gain(ks[27], (N_ODD, QK_DIM)),
        "mla_k_norm": gain(ks[28], (N_ODD, QK_DIM)),
        "odd_w_out": nrm(ks[29], (N_ODD, POOL_W + MLA_HEADS * V_DIM, D), (POOL_W + MLA_HEADS * V_DIM) ** -0.5),
    }


def reference(x_prompt, x_sample, cache_na_k, cache_na_v, cache_mla_ckv, cache_mla_kpe, c, c_ctx,
              ada_w, ada_b, norm1_g, norm2_g, mlp_w1, mlp_w2,
              even_w_in, even_conv_w, na_q_norm, na_k_norm, na_rpb, even_w_out,
              odd_w_in, pool_w, pool_scale, q_a_norm, w_q_b, kv_a_norm, w_kv_b,
              mla_q_norm, mla_k_norm, odd_w_out):
    Bp, Lp, _ = x_prompt.shape
    Bs, Ls, _ = x_sample.shape
    cos, sin = axial_rope(Ls)
    s_ctx = jax.nn.silu(c_ctx)[None, :]
    s_lat = jax.nn.silu(c)
    yp, ys = x_prompt, x_sample
    new_na_k, new_na_v, new_ckv, new_kpe = [], [], [], []
    for l in range(DEPTH):
        sh1p, sc1p, g1p, sh2p, sc2p, g2p = ada_mod(s_ctx, ada_w[l], ada_b[l])
        sh1s, sc1s, g1s, sh2s, sc2s, g2s = ada_mod(s_lat, ada_w[l], ada_b[l])
        hp = modulate(yp, norm1_g[l], sh1p, sc1p)
        hs = modulate(ys, norm1_g[l], sh1s, sc1s)
        i = l // 2
        if l % 2 == 0:
            ap, qp, kp, vp = even_proj(hp, even_w_in[i], even_conv_w[i], na_q_norm[i], na_k_norm[i])
            att_p = dense_attn(qp, kp, vp).reshape(Bp, Lp, NA_W)
            mix_p = jnp.concatenate([ap, att_p], axis=-1) @ even_w_out[i]
            new_na_k.append(kp)
            new_na_v.append(vp)
            a_s, qs, k_s, vs = even_proj(hs, even_w_in[i], even_conv_w[i], na_q_norm[i], na_k_norm[i])
            att_s = na_latent(qs, k_s, vs, cache_na_k[:, i], cache_na_v[:, i], na_rpb[i])
            mix_s = jnp.concatenate([a_s, att_s], axis=-1) @ even_w_out[i]
        else:
            pp, qp, ckvp, kpep = odd_proj(hp, odd_w_in[i], pool_w[i], pool_scale[i], q_a_norm[i],
                                          w_q_b[i], kv_a_norm[i], mla_q_norm[i])
            kp, vp = mla_kv(ckvp, kpep, w_kv_b[i], mla_k_norm[i])
            att_p = dense_attn(qp, kp, vp).reshape(Bp, Lp, MLA_HEADS * V_DIM)
            mix_p = jnp.concatenate([pp, att_p], axis=-1) @ odd_w_out[i]
            new_ckv.append(ckvp)
            new_kpe.append(kpep)
            ps, qs, ckvs, kpes = odd_proj(hs, odd_w_in[i], pool_w[i], pool_scale[i], q_a_norm[i],
                                          w_q_b[i], kv_a_norm[i], mla_q_norm[i])
            k_s, vs = mla_kv(ckvs, kpes, w_kv_b[i], mla_k_norm[i])
            qs = rope_heads(qs, cos, sin)
            k_s = rope_heads(k_s, cos, sin)
            kc, vc = mla_kv(cache_mla_ckv[:, i], cache_mla_kpe[:, i], w_kv_b[i], mla_k_norm[i])
            att_s = dense_attn(qs, jnp.concatenate([kc, k_s], axis=1),
                               jnp.concatenate([vc, vs], axis=1)).reshape(Bs, Ls, MLA_HEADS * V_DIM)
            mix_s = jnp.concatenate([ps, att_s], axis=-1) @ odd_w_out[i]
        yp = yp + g1p * mix_p
        ys = ys + g1s * mix_s
        yp = yp + g2p * sq_relu_mlp(modulate(yp, norm2_g[l], sh2p, sc2p), mlp_w1[l], mlp_w2[l])
        ys = ys + g2s * sq_relu_mlp(modulate(ys, norm2_g[l], sh2s, sc2s), mlp_w1[l], mlp_w2[l])
    return (yp, ys, jnp.stack(new_na_k, axis=1), jnp.stack(new_na_v, axis=1),
            jnp.stack(new_ckv, axis=1), jnp.stack(new_kpe, axis=1))
```

```python
from contextlib import ExitStack
import numpy as np
import concourse.bass as bass
import concourse.mybir as mybir
from concourse.bass_utils import run_bass_kernel_spmd

F32 = mybir.dt.float32
BF16 = mybir.dt.bfloat16
AF = mybir.ActivationFunctionType
ALU = mybir.AluOpType

NCORES = 8
D = 1024
T = 1536
EPS = 1e-6


class Buf:
    __slots__ = ("name", "lw", "rd", "dsem", "dcount")

    def __init__(self, name):
        self.name = name
        self.lw = None
        self.rd = {}
        self.dsem = None
        self.dcount = 0


class Eng:
    def __init__(self, name, handle, sem, key):
        self.name, self.h, self.sem, self.key = name, handle, sem, key
        self.count = 0
        self.ops = []
        self.waited = {}


class Prog:
    EPOCH = 4096

    def __init__(self, nc, stack, same_engine_sync=True):
        self.nc, self.stack = nc, stack
        self.sems = {}
        self.same_engine_sync = same_engine_sync
        self.E = {}
        for name, h in (("pe", nc.tensor), ("act", nc.scalar), ("dve", nc.vector),
                        ("pool", nc.gpsimd), ("sp", nc.sync)):
            sem = stack.enter_context(nc.semaphore("sem_" + name))
            key = "E_" + name
            self.sems[key] = sem
            self.E[name] = Eng(name, h, sem, key)
        self.nbuf = 0
        self.out_events = []

    def buf(self, name=None):
        self.nbuf += 1
        return Buf(name or "b%d" % self.nbuf)

    def bufs(self, n, name="b"):
        return [self.buf("%s%d" % (name, i)) for i in range(n)]

    def _dsem(self, b):
        if b.dsem is None:
            key = "D%d" % len(self.sems)
            self.sems[key] = self.stack.enter_context(self.nc.semaphore(key))
            b.dsem = key
        return b.dsem

    def _deps(self, eng, reads, writes):
        need = {}

        def add(ev, kind):
            if ev is None:
                return
            key, val, ename = ev
            if ename == eng.name:
                if eng.name == "pe" or not self.same_engine_sync:
                    return
            if need.get(key, 0) < val:
                need[key] = val

        for b in reads:
            add(b.lw, "raw")
        for b in writes:
            add(b.lw, "waw")
            for key, (val, ename) in b.rd.items():
                add((key, val, ename), "war")
        waits = []
        for key, val in need.items():
            if eng.waited.get(key, 0) >= val:
                continue
            eng.waited[key] = val
            waits.append((key, val))
        return waits

    def _record(self, ev, reads, writes):
        key, val, ename = ev
        for b in reads:
            old = b.rd.get(key)
            if old is None or old[0] < val:
                b.rd[key] = (val, ename)
        for b in writes:
            b.lw = ev
            b.rd = {}

    def op(self, ename, fn, reads=(), writes=()):
        eng = self.E[ename]
        waits = self._deps(eng, reads, writes)
        ep, idx = divmod(eng.count, self.EPOCH)
        eng.count += 1
        key = "%s_%d" % (eng.key, ep)
        if key not in self.sems:
            self.sems[key] = self.stack.enter_context(self.nc.semaphore(key))
        ev = (key, idx + 1, eng.name)
        eng.ops.append((waits, fn, (key, 1)))
        self._record(ev, reads, writes)
        return ev

    def dma(self, qname, fn, reads=(), writes=(), sbuf=None, is_output=False, chain=False):
        eng = self.E[qname]
        if chain and sbuf.lw is not None and sbuf.lw[0] == sbuf.dsem and not sbuf.rd:
            saved = sbuf.lw
            sbuf.lw = None
            waits = self._deps(eng, reads, writes)
            sbuf.lw = saved
        else:
            waits = self._deps(eng, reads, writes)
        key = self._dsem(sbuf)
        sbuf.dcount += 16
        ev = (key, sbuf.dcount, "dma")
        eng.ops.append((waits, fn, (key, 16)))
        self._record(ev, reads, writes)
        if is_output:
            self.out_events.append(ev)
        return ev

    def finish(self):
        eng = self.E["sp"]
        need = {}
        for key, val, _ in self.out_events:
            if need.get(key, 0) < val:
                need[key] = val
        eng.ops.append((list(need.items()), None, None))

    def emit(self):
        nc, sems = self.nc, self.sems
        with nc.Block() as block:
            def mk(eng):
                def body(h):
                    for waits, fn, inc in eng.ops:
                        for key, val in waits:
                            h.wait_ge(sems[key], val)
                        if fn is None:
                            continue
                        ins = fn(h)
                        if inc is not None:
                            ins.then_inc(sems[inc[0]], inc[1])
                return body
            block.tensor(mk(self.E["pe"]))
            block.scalar(mk(self.E["act"]))
            block.vector(mk(self.E["dve"]))
            block.gpsimd(mk(self.E["pool"]))
            block.sync(mk(self.E["sp"]))


def _const_tables():
    kc = np.arange(64)[:, None]
    qc = np.arange(64)[None, :]
    qcs = np.clip(qc - 8, 0, 48)
    colvalid = ((kc >= qcs) & (kc < qcs + 16)).astype(np.float32)
    MF = np.zeros((128, 14, 64), np.float32)
    MI = np.zeros((128, 14, 64), np.float32)
    for a in range(2):
        for ei in range(14):
            e = 6 - ei
            if -7 <= e + a <= 7:
                MF[a * 64:(a + 1) * 64, ei, :] = colvalid
            if -4 <= e + a <= 3:
                MI[a * 64:(a + 1) * 64, ei, :] = colvalid
    PA = np.zeros((128, 3, 4, 3, 128), np.float32)
    L = 384
    for g, w in enumerate((2, 4, 8, 16)):
        A = np.zeros((L, L), np.float32)
        for t in range(L):
            lo = min(max(t - w // 2, 0), L)
            hi = min(max(t - w // 2 + w, 0), L)
            A[lo:hi, t] = 1.0 / float(hi - lo)
            A[t, t] -= 1.0
        for case, o in enumerate((0, 128, 256)):
            for rel in (-1, 0, 1):
                s = o + rel * 128
                if s < 0 or s + 128 > L:
                    continue
                PA[:, case, g, rel + 1, :] = A[s:s + 128, o:o + 128]
    n = 1024
    t = np.arange(n)
    row = (t // 64).astype(np.float64)
    col = (t % 64).astype(np.float64)
    inv = 1.0 / (10000.0 ** (np.arange(0, 16, 2, dtype=np.float64) / 16.0))
    ang = np.concatenate([row[:, None] * inv, col[:, None] * inv], -1)
    cos, sin = np.cos(ang).astype(np.float32), np.sin(ang).astype(np.float32)
    RC = np.zeros((96, n), np.float32)
    RS = np.zeros((96, n), np.float32)
    RC[64:80] = cos.T
    RC[80:96] = cos.T
    RS[64:80] = sin.T
    RS[80:96] = sin.T
    RM = np.zeros((96, 96), np.float32)
    for r in range(16):
        RM[80 + r, 64 + r] = -1.0
        RM[64 + r, 80 + r] = 1.0
    return dict(MF=MF.reshape(128, 896), MI=MI.reshape(128, 896), PA=PA.reshape(128, 4608),
                RC=RC, RS=RS, RM=RM)


def _rpb_expand(rpb):
    a = np.arange(128)[:, None, None] // 64
    kc = np.arange(128)[:, None, None] % 64
    e = 6 - np.arange(14)[None, :, None]
    qc = np.arange(64)[None, None, :]
    dr = np.clip(e + a + 7, 0, 14) + 0 * qc
    dc = np.clip(kc - qc + 15, 0, 30) + 0 * e
    out = rpb[:, :, dr, dc]
    return np.ascontiguousarray(out.transpose(0, 2, 1, 3, 4)).reshape(2, 128, 8 * 896)


def _fm(v, n):
    lead = v.shape[:-1]
    r = v.reshape(lead + (n, 128))
    return np.ascontiguousarray(np.moveaxis(r, -1, 0))


def build(depth=4):
    nc = bass.Bass("TRN2", target_bir_lowering=False)
    dram = {}

    def din(name, shape):
        dram[name] = nc.dram_tensor(name, list(shape), F32, kind="ExternalInput").ap()
        return dram[name]

    def dout(name, shape):
        dram[name] = nc.dram_tensor(name, list(shape), F32, kind="ExternalOutput").ap()
        return dram[name]

    xT = din("xT", (D, T))
    cT = din("cT", (128, 16))
    ada_w = din("ada_w", (4, D, 6144))
    ada_b = din("ada_b", (128, 4 * 48))
    n1g = din("n1g", (128, 32))
    n2g = din("n2g", (128, 32))
    w1 = din("mlp_w1", (4, D, 4096))
    w2 = din("mlp_w2", (4, 4096, D))
    ewin = din("even_w_in", (2, D, 3072))
    ewout = din("even_w_out", (2, D, D))
    owin = din("odd_w_in", (2, D, 1184))
    owout = din("odd_w_out", (2, D, D))
    convw = din("convw", (128, 24))
    gqna = din("gqna", (128, 2))
    gkna = din("gkna", (128, 2))
    gktok = din("gktok", (128, 128))
    rpbx = din("rpbx", (2, 128, 8 * 896))
    MFd = din("MF", (128, 896))
    MId = din("MI", (128, 896))
    PAd = din("PA", (128, 4608))
    RCd = din("RC", (96, 1024))
    RSd = din("RS", (96, 1024))
    RMd = din("RM", (96, 96))
    poolw = din("poolw", (2, 128, 512))
    pscale = din("pscale", (128, 8))
    qag = din("qag", (128, 6))
    kvag = din("kvag", (128, 4))
    kvgrep = din("kvgrep", (128, 512))
    wqb = din("w_q_b", (2, 384, 768))
    wkvb = din("w_kv_b", (2, 256, 1024))
    gqm = din("gqm", (96, 2))
    gkm = din("gkm", (96, 2))
    nakT = din("nakT", (2, 128, 1024))
    nav = din("nav", (2, 256, 512))
    ckvT = din("ckvT", (2, 256, 256))
    kpeT = din("kpeT", (2, 32, 256))

    yT_out = dout("yT_out", (D, T))
    nak_out = dout("nak_out", (2, 512, 512))
    nav_out = dout("nav_out", (2, 512, 512))
    ckv_out = dout("ckv_out", (2, 512, 256))
    kpe_out = dout("kpe_out", (2, 512, 32))

    with ExitStack() as st:
        P = Prog(nc, st)

        def sb(name, shape, dt=F32):
            return st.enter_context(nc.sbuf_tensor(name, list(shape), dt))

        yT = sb("yT", (128, 8, T))
        hT = sb("hT", (128, 8, T), BF16)
        NW = 2
        WT = [sb("WT%d" % i, (128, 8, 512), BF16) for i in range(NW)]
        bWT = P.bufs(NW, "WT")
        mod = sb("mod", (128, 4, 48, 2))
        A1 = sb("A1", (128, 4, 8, 2))
        A2 = sb("A2", (128, 4, 8, 2))
        ones = sb("ones", (128, 128), BF16)
        bdiag = sb("bdiag", (128, 128), BF16)
        sTb = sb("sTb", (128, 8, 2), BF16)
        c_sb = sb("c_sb", (128, 16))
        adab_sb = sb("adab_sb", (128, 4, 48))
        n1g_sb = sb("n1g_sb", (128, 4, 8))
        n2g_sb = sb("n2g_sb", (128, 4, 8))
        convw_sb = sb("convw_sb", (128, 2, 3, 4))
        gqna_sb = sb("gqna_sb", (128, 2))
        gkna_sb = sb("gkna_sb", (128, 2))
        gktok_sb = sb("gktok_sb", (128, 2, 64))
        pscale_sb = sb("pscale_sb", (128, 2, 4))
        qag_sb = sb("qag_sb", (128, 2, 3))
        kvag_sb = sb("kvag_sb", (128, 2, 2))
        gqm_sb = sb("gqm_sb", (96, 2))
        gkm_sb = sb("gkm_sb", (96, 2))
        eps_sb = sb("eps_sb", (128, 1))
        kvg_sb = sb("kvg_sb", (128, 2, 256))
        rstd = [sb("rstd%d" % i, (128, 512)) for i in range(2)]
        brstd = P.bufs(2, "rstd")
        ARENA_W = 28000
        arena = sb("arena", (128, ARENA_W))
        dummy = sb("gdummy", (128, 2))

        class Arena:
            def __init__(self):
                self.off = 0

            def take(self, shape, dt=F32):
                n = 1
                for s in shape[1:]:
                    n *= s
                words = n if dt == F32 else (n + 1) // 2
                a = arena[0:shape[0], self.off:self.off + words]
                self.off += words
                assert self.off <= ARENA_W, ("arena overflow", self.off)
                if dt != F32:
                    a = a.bitcast(dt)
                if len(shape) == 3:
                    a = a.rearrange("p (a b) -> p a b", a=shape[1])
                elif len(shape) == 4:
                    a = a.rearrange("p (a b c) -> p a b c", a=shape[1], b=shape[2])
                return a

        ps = [st.enter_context(nc.psum_tensor("ps%d" % i, [128, 512], F32)) for i in range(8)]
        bps = P.bufs(8, "ps")
        psn = [0]

        pools = {"rr": [0, 1, 2, 3, 4, 5, 6], "prod": [5, 6], "att": [3, 4]}
        pcnt = {"rr": 0, "prod": 0, "att": 0}

        def _take(pool):
            lst = pools[pool]
            i = lst[pcnt[pool] % len(lst)]
            pcnt[pool] += 1
            return ps[i], bps[i]

        def nps():
            return _take("rr")

        def hps():
            return _take("prod")

        def aps():
            return _take("att")

        def handover(prev_bufs, new_bufs):
            g = P.buf("guard")
            P.op("dve", lambda h: h.memset(dummy[:], 0.0), writes=list(prev_bufs) + [g])
            for b in new_bufs:
                b.lw = g.lw
                b.rd = {}

        by = P.bufs(3, "y")
        bh = P.bufs(3, "h")
        bconst = P.buf("const")
        bmodA = P.bufs(4, "modA")
        bmodB = P.bufs(4, "modB")

        def blk(tb):
            return slice(tb * 512, (tb + 1) * 512)

        def mcol(tb):
            return 0 if tb == 0 else 1

        def MM(out, lhsT, rhs, start, stop, reads, writes):
            P.op("pe", lambda h: h.matmul(out, lhsT=lhsT, rhs=rhs, start=start, stop=stop),
                 reads=reads, writes=writes)

        def ACT(out, in_, func, reads, writes, bias=None, scale=None):
            kw = {}
            if bias is not None:
                kw["bias"] = bias
            if scale is not None:
                kw["scale"] = scale
            P.op("act", lambda h: h.activation(out=out, in_=in_, func=func, **kw), reads=reads, writes=writes)

        def TT(out, in0, in1, op, reads, writes, eng="dve"):
            P.op(eng, lambda h: h.tensor_tensor(out=out, in0=in0, in1=in1, op=op), reads=reads, writes=writes)

        def TS(out, in0, s1, s2, op0, op1, reads, writes, eng="dve"):
            if op1 is None:
                P.op(eng, lambda h: h.tensor_scalar(out=out, in0=in0, scalar1=s1, scalar2=None, op0=op0),
                     reads=reads, writes=writes)
            else:
                P.op(eng, lambda h: h.tensor_scalar(out=out, in0=in0, scalar1=s1, scalar2=s2, op0=op0, op1=op1),
                     reads=reads, writes=writes)

        def STT(out, in0, scalar, in1, op0, op1, reads, writes):
            P.op("dve", lambda h: h.scalar_tensor_tensor(out=out, in0=in0, scalar=scalar, in1=in1, op0=op0, op1=op1),
                 reads=reads, writes=writes)

        def RECIP(out, in_, reads, writes):
            P.op("dve", lambda h: h.reciprocal(out=out, in_=in_), reads=reads, writes=writes)

        def COPY(out, in_, reads, writes, eng="dve"):
            if eng == "act":
                ACT(out, in_, AF.Copy, reads, writes)
            else:
                P.op(eng, lambda h: h.tensor_copy(out=out, in_=in_), reads=reads, writes=writes)

        def LOAD(out, in_, b, q="sp", chain=False):
            P.dma(q, lambda h: h.dma_start(out=out, in_=in_), writes=[b], sbuf=b, chain=chain)

        def STORE(out, in_, b):
            P.dma("sp", lambda h: h.dma_start(out=out, in_=in_), reads=[b], sbuf=b, is_output=True)

        wt_i = [0]

        def wtile(srcs):
            s = wt_i[0] % NW
            wt_i[0] += 1
            for (c0, ncol, src) in srcs:
                nkc = src.shape[1]
                P.dma("pool", lambda h, s=s, c0=c0, ncol=ncol, src=src, nkc=nkc:
                      h.dma_start(out=WT[s][:, 0:nkc, c0:c0 + ncol], in_=src),
                      writes=[bWT[s]], sbuf=bWT[s])
            return WT[s], bWT[s]

        def kview(w2d):
            return w2d.rearrange("(kc k) n -> k kc n", k=128)

        rs_i = [0]

        def rstd_from(ss_ps, bss, npart, ncol, inv_n):
            i = rs_i[0] % len(rstd)
            rs_i[0] += 1
            r, br = rstd[i], brstd[i]
            ACT(r[0:npart, 0:ncol], ss_ps, AF.Ln, [bss, bconst], [br], bias=eps_sb[0:npart, :], scale=inv_n)
            ACT(r[0:npart, 0:ncol], r[0:npart, 0:ncol], AF.Exp, [br], [br], scale=-0.5)
            return r[0:npart, 0:ncol], br

        bsm = P.buf("small")
        LOAD(c_sb[:], cT, bsm, chain=True)
        LOAD(adab_sb[:], ada_b.rearrange("p (l n) -> p l n", l=4), bsm, chain=True)
        LOAD(n1g_sb[:], n1g.rearrange("p (l n) -> p l n", l=4), bsm, chain=True)
        LOAD(n2g_sb[:], n2g.rearrange("p (l n) -> p l n", l=4), bsm, chain=True)
        LOAD(convw_sb[:], convw.rearrange("p (i t c) -> p i t c", i=2, t=3), bsm, chain=True)
        LOAD(gqna_sb[:], gqna, bsm, chain=True)
        LOAD(gkna_sb[:], gkna, bsm, chain=True)
        LOAD(gktok_sb[:], gktok.rearrange("p (i d) -> p i d", i=2), bsm, chain=True)
        LOAD(pscale_sb[:], pscale.rearrange("p (i c) -> p i c", i=2), bsm, chain=True)
        LOAD(qag_sb[:], qag.rearrange("p (i c) -> p i c", i=2), bsm, chain=True)
        LOAD(kvag_sb[:], kvag.rearrange("p (i c) -> p i c", i=2), bsm, chain=True)
        LOAD(gqm_sb[:], gqm, bsm, chain=True)
        LOAD(gkm_sb[:], gkm, bsm, chain=True)
        LOAD(kvg_sb[:], kvgrep.rearrange("p (i d) -> p i d", i=2), bsm, chain=True)
        LOAD(yT[:, :, blk(0)], kview(xT)[:, :, blk(0)], by[0])
        P.op("dve", lambda h: h.memset(ones[:], 1.0), writes=[bconst])
        P.op("dve", lambda h: h.memset(bdiag[:], 0.0), writes=[bconst])
        P.op("dve", lambda h: h.memset(bdiag[0:64, 0:64], 1.0), writes=[bconst])
        P.op("dve", lambda h: h.memset(bdiag[64:128, 64:128], 1.0), writes=[bconst])
        P.op("dve", lambda h: h.memset(eps_sb[:], EPS), writes=[bconst])
        TS(gqna_sb[:], gqna_sb[:], 0.125, None, ALU.mult, None, [bsm], [bsm])
        TS(gqm_sb[:], gqm_sb[:], float(96 ** -0.5), None, ALU.mult, None, [bsm], [bsm])
        ACT(sTb[:].rearrange("p k m -> p (k m)"), c_sb[:], AF.Silu, [bsm], [bsm])

        def ada_steps(l, slots=None, split=False):
            ptB, bptB = ps[7], bps[7]
            ptA, bptA = (ps[6], bps[6]) if split else (ptB, bptB)
            stepsA, stepsB = [], []
            for nt in range(12):
                def st_(nt=nt):
                    pt, bpt = (ptA, bptA) if nt < 4 else (ptB, bptB)
                    if slots is None:
                        w, bw = wtile([(0, 512, kview(ada_w[l])[:, :, nt * 512:(nt + 1) * 512])])
                    else:
                        w, bw = slots[nt % len(slots)]
                        P.dma("pool", lambda h, w=w, nt=nt: h.dma_start(out=w[:], in_=kview(ada_w[l])[:, :, nt * 512:(nt + 1) * 512]),
                              writes=[bw], sbuf=bw)
                    for j in range(4):
                        n = nt * 4 + j
                        for kc in range(8):
                            MM(pt[:, 2 * n:2 * n + 2], w[:, kc, j * 128:(j + 1) * 128], sTb[:, kc, :],
                               kc == 0, kc == 7, [bw, bsm], [bpt])
                (stepsA if nt < 4 else stepsB).append(st_)

            def finA():
                pv = ptA[:, 0:32].rearrange("p (n m) -> p n m", m=2)
                for m in range(2):
                    TT(mod[:, l, 0:16, m], pv[:, :, m], adab_sb[:, l, 0:16], ALU.add, [bptA, bsm], [bmodA[l]])
                for m in range(2):
                    STT(A1[:, l, :, m], mod[:, l, 8:16, m], 1.0, n1g_sb[:, l, :], ALU.add, ALU.mult,
                        [bmodA[l], bsm], [bmodA[l]])

            def finB():
                pv = ptB[:, 32:96].rearrange("p (n m) -> p n m", m=2)
                for m in range(2):
                    TT(mod[:, l, 16:48, m], pv[:, :, m], adab_sb[:, l, 16:48], ALU.add, [bptB, bsm], [bmodB[l]])
                for m in range(2):
                    STT(A2[:, l, :, m], mod[:, l, 32:40, m], 1.0, n2g_sb[:, l, :], ALU.add, ALU.mult,
                        [bmodB[l], bsm], [bmodB[l]])
            if split:
                return stepsA + [finA], stepsB + [finB]
            return stepsA + stepsB + [finA, finB]

        def norm_parts(l, which, tb, sq, bsq, tmp, btmp):
            Asc = A1 if which == 1 else A2
            sh0 = 0 if which == 1 else 24
            m = mcol(tb)
            bmw = bmodA[l] if which == 1 else bmodB[l]

            def pa():
                ACT(sq[:], yT[:, :, blk(tb)], AF.Square, [by[tb]], [bsq])

            def pb():
                pt, bpt = nps()
                for kc in range(8):
                    MM(pt[:], ones[:], sq[:, kc, :], kc == 0, kc == 7, [bsq, bconst], [bpt])
                r, br = rstd_from(pt[:], bpt, 128, 512, 1.0 / D)
                for kc in range(8):
                    STT(tmp[:, kc % 2, :], yT[:, kc, blk(tb)], Asc[:, l, kc, m:m + 1], r, ALU.mult, ALU.mult,
                        [by[tb], bmw, br], [btmp[kc % 2]])
                    ACT(hT[:, kc, blk(tb)], tmp[:, kc % 2, :], AF.Identity, [btmp[kc % 2], bmw], [bh[tb]],
                        bias=mod[:, l, sh0 + kc, m:m + 1])
            return pa, pb

        def norm_mod(l, which, sq, bsq, tmp, btmp):
            for tb in range(3):
                pa, pb = norm_parts(l, which, tb, sq, bsq, tmp, btmp)
                pa()
                pb()

        def pnorm_start(pt, bpt, npart, ncol, onesm, inv_n, gain, out, outbufs, sqt, bsqt, extra_reads=(), post=None):
            ACT(sqt[0:npart, 0:ncol], pt[0:npart, 0:ncol], AF.Square, [bpt], [bsqt])

            def fin():
                p2, bp2 = nps()
                MM(p2[0:npart, 0:ncol], onesm, sqt[0:npart, 0:ncol], True, True, [bsqt, bconst], [bp2])
                r, br = rstd_from(p2[0:npart, 0:ncol], bp2, npart, ncol, inv_n)
                if isinstance(out, list):
                    for (p0, p1, o_) in out:
                        STT(o_, pt[p0:p1, 0:ncol], gain[p0:p1, :], r[p0:p1, :], ALU.mult, ALU.mult,
                            [bpt, br, bsm] + list(extra_reads), outbufs)
                else:
                    STT(out, pt[0:npart, 0:ncol], gain, r, ALU.mult, ALU.mult, [bpt, br, bsm] + list(extra_reads), outbufs)
                if post is not None:
                    post()
            return fin

        def merge(A, B):
            ia = ib = 0
            while ia < len(A) or ib < len(B):
                if ia < len(A) and (ib >= len(B) or ia * len(B) <= ib * len(A)):
                    A[ia]()
                    ia += 1
                else:
                    B[ib]()
                    ib += 1

        def mlp(l, prev_bufs, with_ada):
            ar = Arena()
            hid = [ar.take((128, 8, T), BF16) for _ in range(2)]
            msk = [ar.take((128, 512)) for _ in range(2)]
            AW = [ar.take((128, 8, 512), BF16) for _ in range(2)]
            sqn = ar.take((128, 8, 512), BF16)
            tmpn = ar.take((128, 2, 512))
            bhid = [P.bufs(3) for _ in range(2)]
            bmsk = P.bufs(2)
            bAW = P.bufs(2, "AW")
            bsqn, btmpn = P.buf(), P.bufs(2)
            allb = sum(bhid, []) + bmsk + bAW + [bsqn] + btmpn
            handover(prev_bufs, allb)
            extra = ada_steps(l + 1, list(zip(AW, bAW))) if with_ada else []
            k = 0
            for qd in range(4):
                hq, bhq = hid[qd % 2], bhid[qd % 2]
                for half in range(2):
                    c0 = qd * 1024 + half * 512
                    w, bw = wtile([(0, 512, kview(w1[l])[:, :, c0:c0 + 512])])
                    for j in range(4):
                        jj = half * 4 + j
                        for tb in range(3):
                            pt, bpt = nps()
                            for kc in range(8):
                                MM(pt[:], w[:, kc, j * 128:(j + 1) * 128], hT[:, kc, blk(tb)], kc == 0, kc == 7,
                                   [bw, bh[tb]], [bpt])
                            mi = k % 2
                            k += 1
                            ACT(msk[mi][:], pt[:], AF.Square, [bpt], [bmsk[mi]])
                            STT(hq[:, jj, blk(tb)], pt[:], 0.0, msk[mi][:], ALU.is_gt, ALU.mult,
                                [bpt, bmsk[mi]], [bhq[tb]])
                    if extra:
                        extra.pop(0)()
                if qd == 3:
                    while extra:
                        extra.pop(0)()
                    ws = [wtile([(0, 512, kview(w2[l][qd * 1024:(qd + 1) * 1024, :])[:, :, half * 512:(half + 1) * 512])])
                          for half in range(2)]
                    pend = None
                    for tb in range(3):
                        m = mcol(tb)
                        for half in range(2):
                            w, bw = ws[half]
                            for j in range(4):
                                n = half * 4 + j
                                pt, bpt = nps()
                                for kc in range(8):
                                    MM(pt[:], w[:, kc, j * 128:(j + 1) * 128], hq[:, kc, blk(tb)], kc == 0, kc == 7,
                                       [bw, bhq[tb]], [bpt])
                                STT(yT[:, n, blk(tb)], pt[:], mod[:, l, 40 + n, m:m + 1], yT[:, n, blk(tb)],
                                    ALU.mult, ALU.add, [bpt, bmodB[l], by[tb]], [by[tb]])
                        if with_ada:
                            pa, pb = norm_parts(l + 1, 1, tb, sqn, bsqn, tmpn, btmpn)
                            if pend is not None:
                                pend()
                            pa()
                            pend = pb
                        else:
                            P.dma("sp", lambda h, tb=tb: h.dma_start(out=kview(yT_out)[:, :, blk(tb)], in_=yT[:, :, blk(tb)]),
                                  reads=[by[tb]], sbuf=by[tb], is_output=True)
                    if pend is not None:
                        pend()
                    continue
                for half in range(2):
                    w, bw = wtile([(0, 512, kview(w2[l][qd * 1024:(qd + 1) * 1024, :])[:, :, half * 512:(half + 1) * 512])])
                    for j in range(4):
                        n = half * 4 + j
                        for tb in range(3):
                            m = mcol(tb)
                            pt, bpt = nps()
                            for kc in range(8):
                                MM(pt[:], w[:, kc, j * 128:(j + 1) * 128], hq[:, kc, blk(tb)], kc == 0, kc == 7,
                                   [bw, bhq[tb]], [bpt])
                            STT(yT[:, n, blk(tb)], pt[:], mod[:, l, 40 + n, m:m + 1], yT[:, n, blk(tb)],
                                ALU.mult, ALU.add, [bpt, bmodB[l], by[tb]], [by[tb]])
                    if extra and qd < 3:
                        extra.pop(0)()
            while extra:
                extra.pop(0)()
            return allb

        def out_proj(l, wsrc, lo, blo, hi, bhi, sq, bsq, tmp, btmp):
            ws = [wtile([(0, 512, kview(wsrc)[:, :, nt * 512:(nt + 1) * 512])]) for nt in range(2)]
            pend = None
            for tb in range(3):
                m = mcol(tb)
                for nt in range(2):
                    w, bw = ws[nt]
                    for j in range(4):
                        n = nt * 4 + j
                        pt, bpt = nps()
                        for kc in range(8):
                            src = lo[:, kc, blk(tb)] if kc < 4 else hi[:, kc - 4, blk(tb)]
                            MM(pt[:], w[:, kc, j * 128:(j + 1) * 128], src, kc == 0, kc == 7,
                               [bw, blo[tb], bhi[tb]], [bpt])
                        STT(yT[:, n, blk(tb)], pt[:], mod[:, l, 16 + n, m:m + 1], yT[:, n, blk(tb)],
                            ALU.mult, ALU.add, [bpt, bmodB[l], by[tb]], [by[tb]])
                pa, pb = norm_parts(l, 2, tb, sq, bsq, tmp, btmp)
                if pend is not None:
                    pend()
                pa()
                pend = pb
            pend()

        def score_steps(u, PT, bPT):
            tiles, ncq = u["tiles"], u["ncq"]
            nt_ = len(tiles)
            per = 512 // ncq
            steps = []
            for i in range(0, nt_, per):
                def st_(i=i):
                    if i == 0 and u.get("pre") is not None:
                        u["pre"]()
                    grp = tiles[i:i + per]
                    pt, bpt = nps()
                    for g, tl in enumerate(grp):
                        MM(pt[:, g * ncq:(g + 1) * ncq], tl[0], tl[1], True, True, tl[2], [bpt])
                    w = len(grp) * ncq
                    ACT(PT[:, i * ncq:i * ncq + w], pt[:, 0:w], AF.Exp, [bpt], [bPT[i + g] for g in range(len(grp))])
                    for g, tl in enumerate(grp):
                        if tl[5] is not None:
                            base = (i + g) * ncq
                            for (c0_, n_, tab) in tl[5]:
                                dst = PT[:, base + c0_:base + c0_ + n_]
                                if tab is None:
                                    P.op("dve", lambda h, dst=dst: h.memset(dst, 0.0), reads=[], writes=[bPT[i + g]])
                                else:
                                    TT(dst, dst, tab, ALU.mult, [bPT[i + g]] + tl[6], [bPT[i + g]])
                steps.append(st_)
            return steps

        def finish_steps(u, PT, bPT):
            tiles, ncq, b0 = u["tiles"], u["ncq"], u["b0"]
            nt_ = len(tiles)
            box = {}
            steps = []
            for i in range(nt_):
                def st_(i=i):
                    if i == 0:
                        box["po"] = aps()
                        box["pd"] = aps()
                    (po, bpo), (pd, bpd) = box["po"], box["pd"]
                    tl = tiles[i]
                    MM(po[:, 0:ncq], tl[3], PT[:, i * ncq:(i + 1) * ncq], i == 0, i == nt_ - 1, [bPT[i]] + tl[4], [bpo])
                    MM(pd[:, 0:ncq], ones[:, :], PT[:, i * ncq:(i + 1) * ncq], i == 0, i == nt_ - 1, [bPT[i], bconst], [bpd])
                    if i == nt_ - 1:
                        ri = rs_i[0] % len(rstd)
                        rs_i[0] += 1
                        r, br = rstd[ri], brstd[ri]
                        ACT(r[b0:b0 + 64, 0:ncq], pd[b0:b0 + 64, 0:ncq], AF.Ln, [bpd], [br])
                        ACT(r[b0:b0 + 64, 0:ncq], r[b0:b0 + 64, 0:ncq], AF.Exp, [br], [br], scale=-1.0)
                        TT(u["out"], po[b0:b0 + 64, 0:ncq], r[b0:b0 + 64, 0:ncq], ALU.mult, [bpo, br], [u["outbuf"]])
                steps.append(st_)
            return steps

        upi = [0]

        def merged(A, B):
            out = []
            ia = ib = 0
            while ia < len(A) or ib < len(B):
                if ia < len(A) and (ib >= len(B) or ia * len(B) <= ib * len(A)):
                    out.append(A[ia])
                    ia += 1
                else:
                    out.append(B[ib])
                    ib += 1
            return out

        def unit_steps(units, PTs, bPTs):
            steps = []
            pend = []
            for u in units:
                k = upi[0]
                upi[0] += 1
                pt_, bpt_ = PTs[k % len(PTs)], bPTs[k % len(PTs)]
                steps += merged(score_steps(u, pt_, bpt_), pend)
                pend = finish_steps(u, pt_, bpt_)
            steps += pend
            return steps

        def even_layer(l, prev_bufs, extra_steps):
            i = l // 2
            ar = Arena()
            aT = ar.take((128, 4, T), BF16)
            qTm = ar.take((128, 8, T), BF16)
            kT = ar.take((128, 4, T), BF16)
            kcT = ar.take((128, 4, 256), BF16)
            V = ar.take((128, 14, 512), BF16)
            sqt2 = [ar.take((128, 512), BF16) for _ in range(2)]
            MFs = ar.take((128, 896), BF16)
            MIs = ar.take((128, 896), BF16)
            ost = [ar.take((128, 512)) for _ in range(2)]
            sst = ar.take((128, 8))
            rx = [ar.take((128, 512)) for _ in range(2)]
            u0 = ar.off
            sq = ar.take((128, 8, 512), BF16)
            tmp = ar.take((128, 2, 512))
            cgb = ar.take((128, T))
            ub = ar.take((128, T))
            ar.off = u0
            PT = [ar.take((128, 4096), BF16) for _ in range(2)]
            stg = [ar.take((128, 896)) for _ in range(2)]
            Gf = [ar.take((128, 896), BF16) for _ in range(2)]
            Gi = [ar.take((128, 896), BF16) for _ in range(2)]
            attT = hT[:, 0:4, :]
            bsq, btmp = P.buf(), P.bufs(2)
            baT, bq, bk = P.bufs(3, "aT"), P.bufs(3, "q"), P.bufs(3, "k")
            bkc, bV, bVc = P.buf("kc"), P.bufs(3, "V"), P.buf("Vc")
            bcg, bub, bsqt2 = P.buf(), P.buf(), P.bufs(2)
            bPT, bstg, bG = [P.bufs(8, "PTa"), P.bufs(8, "PTb")], P.bufs(2), P.bufs(2)
            bM, bost, bsst = P.buf(), P.bufs(2), P.buf()
            brx = P.bufs(2, "rx")
            phA = [bsq] + btmp + [bcg, bub]
            phC = bPT[0] + bPT[1] + bstg + bG
            allb = phA + baT + bq + bk + [bkc] + bV + [bVc, bM] + bsqt2 + bost + [bsst] + brx + phC
            handover(prev_bufs, [b for b in allb if b not in phC])
            rstd.extend(rx)
            brstd.extend(brx)


            if l == 0:
                norm_mod(l, 1, sq, bsq, tmp, btmp)

            W = kview(ewin[i])
            cw = convw_sb
            segs = [(0, 256), (256, 512), (512, 1536)]
            for j in range(4):
                w, bw = wtile([(0, 128, W[:, :, j * 128:(j + 1) * 128]),
                               (128, 128, W[:, :, 512 + j * 128:512 + (j + 1) * 128]),
                               (256, 128, W[:, :, 1024 + j * 128:1024 + (j + 1) * 128])])
                for tb in range(3):
                    pt, bpt = nps()
                    for kc in range(8):
                        MM(pt[:], w[:, kc, 128:256], hT[:, kc, blk(tb)], kc == 0, kc == 7, [bw, bh[tb]], [bpt])
                    COPY(cgb[:, blk(tb)], pt[:], [bpt], [bcg], eng="act")
                for tb in range(3):
                    pt, bpt = nps()
                    for kc in range(8):
                        MM(pt[:], w[:, kc, 256:384], hT[:, kc, blk(tb)], kc == 0, kc == 7, [bw, bh[tb]], [bpt])
                    TT(ub[:, blk(tb)], pt[:], cgb[:, blk(tb)], ALU.mult, [bpt, bcg], [bub])
                ACT(cgb[:], ub[:], AF.Identity, [bub, bsm], [bcg], scale=cw[:, i, 1, j:j + 1])
                for (s0, s1) in segs:
                    STT(cgb[:, s0 + 1:s1], ub[:, s0:s1 - 1], cw[:, i, 0, j:j + 1], cgb[:, s0 + 1:s1],
                        ALU.mult, ALU.add, [bub, bcg, bsm], [bcg])
                    STT(cgb[:, s0:s1 - 1], ub[:, s0 + 1:s1], cw[:, i, 2, j:j + 1], cgb[:, s0:s1 - 1],
                        ALU.mult, ALU.add, [bub, bcg, bsm], [bcg])
                for tb in range(3):
                    pt, bpt = nps()
                    for kc in range(8):
                        MM(pt[:], w[:, kc, 0:128], hT[:, kc, blk(tb)], kc == 0, kc == 7, [bw, bh[tb]], [bpt])
                    TT(aT[:, j, blk(tb)], pt[:], cgb[:, blk(tb)], ALU.mult, [bpt, bcg], [baT[tb]])

            LOAD(kcT[:], nakT[i].rearrange("p (c t) -> p c t", c=4), bkc, q="pool")
            LOAD(V[:, 12:14, :], nav[i].rearrange("(c p) n -> p c n", p=128), bVc, q="pool")
            LOAD(MFs[:], MFd, bM, q="pool")
            LOAD(MIs[:], MId, bM, q="pool")
            for which, dst, bdst, gain in ((3, None, bq, gqna_sb), (4, kT, bk, gkna_sb)):
                w, bw = wtile([(0, 512, W[:, :, which * 512:(which + 1) * 512])])
                pend = None
                kk = 0
                for j in range(4):
                    for tb in range(3):
                        pt, bpt = nps()
                        for kc in range(8):
                            MM(pt[:], w[:, kc, j * 128:(j + 1) * 128], hT[:, kc, blk(tb)], kc == 0, kc == 7,
                               [bw, bh[tb]], [bpt])
                        if which == 3:
                            o_ = [(0, 64, qTm[0:64, 2 * j, blk(tb)]), (64, 128, qTm[64:128, 2 * j + 1, blk(tb)])]
                        else:
                            o_ = dst[:, j, blk(tb)]
                        fin = pnorm_start(pt, bpt, 128, 512, bdiag[:], 1.0 / 64, gain[:, i:i + 1], o_,
                                          [bdst[tb]], sqt2[kk % 2], bsqt2[kk % 2])
                        kk += 1
                        if pend is not None:
                            pend()
                        pend = fin
                pend()
                if which == 4:
                    for tt in range(4):
                        pt, bpt = nps()
                        for kc in range(8):
                            MM(pt[:], hT[:, kc, tt * 128:(tt + 1) * 128], w[:, kc, :], kc == 0, kc == 7,
                               [bw, bh[0]], [bpt])
                        o, bo = ost[tt % 2], bost[tt % 2]
                        ACT(o[:], pt[:], AF.Square, [bpt], [bo])
                        P.op("dve", lambda h, o=o: h.tensor_reduce(out=sst[:, 0:8], in_=o[:].rearrange("p (h d) -> p h d", h=8),
                                                                   axis=mybir.AxisListType.X, op=ALU.add),
                             reads=[bo], writes=[bsst])
                        ACT(sst[:, 0:8], sst[:, 0:8], AF.Ln, [bsst, bconst], [bsst], bias=eps_sb[:, :], scale=1.0 / 64)
                        ACT(sst[:, 0:8], sst[:, 0:8], AF.Exp, [bsst], [bsst], scale=-0.5)
                        ov = o[:].rearrange("p (h d) -> p h d", h=8)
                        pv = pt[:].rearrange("p (h d) -> p h d", h=8)
                        for hh in range(8):
                            STT(ov[:, hh, :], pv[:, hh, :], sst[:, hh:hh + 1], gktok_sb[:, i, :], ALU.mult, ALU.mult,
                                [bpt, bsst, bsm], [bo])
                        STORE(nak_out[i, tt * 128:(tt + 1) * 128, :], o[:], bo)
            w, bw = wtile([(0, 512, W[:, :, 2560:3072])])
            qv = qTm.rearrange("p (j two) t -> p j two t", two=2)
            for tb_ in range(3):
                P.op("pool", lambda h, tb_=tb_: h.memset(qv[64:128, :, 0, blk(tb_)], 0.0), writes=[bq[tb_]])
                P.op("pool", lambda h, tb_=tb_: h.memset(qv[0:64, :, 1, blk(tb_)], 0.0), writes=[bq[tb_]])
            for tt in range(12):
                pt, bpt = nps()
                for kc in range(8):
                    MM(pt[:], hT[:, kc, tt * 128:(tt + 1) * 128], w[:, kc, :], kc == 0, kc == 7,
                       [bw, bh[tt // 4]], [bpt])
                COPY(V[:, tt, :], pt[:], [bpt], [bV[tt // 4]], eng="act")
                if tt < 4:
                    o, bo = ost[tt % 2], bost[tt % 2]
                    COPY(o[:], pt[:], [bpt], [bo])
                    STORE(nav_out[i, tt * 128:(tt + 1) * 128, :], o[:], bo)

            handover(phA, phC)
            batt = bh
            units = []
            for s_ in range(2):
                for hh in range(8):
                    j, b0 = hh // 2, (hh % 2) * 64
                    tiles = []
                    for c2 in range(2):
                        k0 = s_ * 256 + c2 * 128
                        tiles.append((kT[:, j, k0:k0 + 128], qTm[:, hh, s_ * 256:(s_ + 1) * 256],
                                      [bk[0], bq[0]], V[:, s_ * 2 + c2, j * 128:(j + 1) * 128], [bV[0]], None, None))
                    units.append(dict(tiles=tiles, b0=b0, out=attT[b0:b0 + 64, j, s_ * 256:(s_ + 1) * 256],
                                      outbuf=batt[0], ncq=256))
            chunks = [[0, 1, 2, 3], [0, 1, 2, 3, 4, 5], [2, 3, 4, 5, 6, 7], [4, 5, 6, 7]]
            for hh in range(8):
                j, b0 = hh // 2, (hh % 2) * 64
                g = hh % 2

                def pre(hh=hh, g=g):
                    LOAD(stg[g][:], rpbx[i][:, hh * 896:(hh + 1) * 896], bstg[g])
                    ACT(stg[g][:], stg[g][:], AF.Exp, [bstg[g]], [bstg[g]])
                    TT(Gf[g][:], stg[g][:], MFs[:], ALU.mult, [bstg[g], bM], [bG[g]])
                    TT(Gi[g][:], stg[g][:], MIs[:], ALU.mult, [bstg[g], bM], [bG[g]])
                for qh in range(2):
                    tbq = 1 + qh
                    q0 = 512 + qh * 512
                    tiles = []
                    for c in ([0, 1, 2, 3, 4, 5] if qh == 0 else [2, 3, 4, 5, 6, 7]):
                        k0 = 512 + c * 128
                        if qh == 0:
                            subs = [(0, 256, Gf[g][:, (6 - 2 * c) * 64:(10 - 2 * c) * 64] if c <= 3 else None),
                                    (256, 256, Gi[g][:, (10 - 2 * c) * 64:(14 - 2 * c) * 64])]
                        else:
                            subs = [(0, 256, Gi[g][:, (14 - 2 * c) * 64:(18 - 2 * c) * 64]),
                                    (256, 256, Gf[g][:, (18 - 2 * c) * 64:(22 - 2 * c) * 64] if c >= 4 else None)]
                        tiles.append((kT[:, j, k0:k0 + 128], qTm[:, hh, q0:q0 + 512],
                                      [bk[1 + c // 4], bq[tbq]], V[:, 4 + c, j * 128:(j + 1) * 128], [bV[1 + c // 4]],
                                      subs, [bG[g]]))
                    for c2 in range(2):
                        tiles.append((kcT[:, j, c2 * 128:(c2 + 1) * 128], qTm[:, hh, q0:q0 + 512],
                                      [bkc, bq[tbq]], V[:, 12 + c2, j * 128:(j + 1) * 128], [bVc], None, None))
                    units.append(dict(tiles=tiles, b0=b0, out=attT[b0:b0 + 64, j, q0:q0 + 512], outbuf=batt[tbq],
                                      ncq=512))
                    if qh == 0:
                        units[max(0, len(units) - 3)]["pre"] = pre
            pools["rr"] = [0, 1, 2, 5, 6]
            for st_ in merged(unit_steps(units, PT, bPT), extra_steps):
                st_()
            pools["rr"] = [0, 1, 2, 3, 4, 5, 6]
            handover(phC, [bsq] + btmp)
            out_proj(l, ewout[i], aT, baT, attT, batt, sq, bsq, tmp, btmp)
            del rstd[2:]
            del brstd[2:]
            return allb

        def odd_layer(l, prev_bufs):
            i = l // 2
            ar = Arena()
            poolT = ar.take((128, 4, T), BF16)
            qln = ar.take((128, 3, T), BF16)
            ckn = ar.take((128, 2, 1792), BF16)
            kpe = ar.take((96, 1792))
            sqK = ar.take((96, 1792), BF16)
            Vall = ar.take((128, 14, 512), BF16)
            wkv = ar.take((128, 2, 1024), BF16)
            RC = ar.take((96, 1024))
            RS = ar.take((96, 1024))
            RM = ar.take((96, 96), BF16)
            sqt = ar.take((128, 3, 512), BF16)
            sst = ar.take((128, 8))
            u0 = ar.off
            sq = ar.take((128, 8, 512), BF16)
            tmp = ar.take((128, 2, 512))
            ar.off = u0
            utok = ar.take((128, 12, 512), BF16)
            diffT = ar.take((128, 4, T), BF16)
            PAs = ar.take((128, 36, 128), BF16)
            pw = ar.take((128, 4, 128), BF16)
            ost = [ar.take((128, 288)) for _ in range(2)]
            w2s = ar.take((128, 8, 160), BF16)
            ar.off = u0
            PT = [ar.take((128, 5120), BF16) for _ in range(2)]
            QT = [ar.take((96, T), BF16) for _ in range(2)]
            KT = [ar.take((96, 1792), BF16) for _ in range(2)]
            rt = [ar.take((96, 512)) for _ in range(2)]
            wq = ar.take((128, 3, 768), BF16)
            attT = hT[:, 0:4, :]
            ones96 = ones[0:96, 0:96]

            bsq, btmp = P.buf(), P.bufs(2)
            bpoolT, bqln, bckn = P.bufs(3, "poolT"), P.bufs(3, "qln"), P.bufs(4, "ckn")
            bkpe, bsqK, bVall = P.bufs(4, "kpe"), P.bufs(4, "sqK"), P.bufs(4, "Vall")
            bw_ = P.buf("wsm")
            bwB, bwC, bwR, bw2s = P.buf("wB"), P.buf("wC"), P.buf("wR"), P.buf("w2s")
            bsqt3, bQT, bKT, brt = P.bufs(3), P.bufs(2), P.bufs(2), P.bufs(2)
            bost, bsst = P.bufs(2), P.buf()
            butok, bdiff, bPT = P.bufs(3, "utok"), P.bufs(3, "diff"), [P.bufs(10, "PTa"), P.bufs(10, "PTb")]
            phA = [bsq] + btmp
            phB = butok + bdiff + [bwB, bw2s] + bost
            phC = bPT[0] + bPT[1] + bQT + bKT + brt + [bwC]
            allb = (phA + bpoolT + bqln + bckn + bkpe + bsqK + bVall + [bw_, bwR] + bsqt3 + [bsst] + phB + phC)
            handover(prev_bufs, [b for b in allb if b not in phB and b not in phC])

            W = kview(owin[i])
            w0, bw0 = wtile([(0, 512, W[:, :, 0:512])])
            w1_, bw1 = wtile([(0, 512, W[:, :, 512:1024])])
            LOAD(wkv[:], kview(wkvb[i]), bw_, q="pool")
            LOAD(RM[:], RMd, bw_, q="pool")
            LOAD(RC[:], RCd, bwR)
            LOAD(RS[:], RSd, bwR)
            LOAD(ckn[:, :, 0:256], kview(ckvT[i]), bckn[0], q="pool")
            LOAD(kpe[64:96, 0:256], kpeT[i], bkpe[0])

            W = kview(owin[i])
            handover(phA, phB)
            LOAD(w2s[:], W[:, :, 1024:1184], bw2s, q="pool")
            LOAD(pw[:], poolw[i].rearrange("p (g d) -> p g d", g=4), bwB, q="pool")
            for s3 in range(3):
                LOAD(PAs[:, s3 * 12:(s3 + 1) * 12, :], PAd[:, s3 * 1536:(s3 + 1) * 1536].rearrange("p (a t) -> p a t", a=12),
                     bwB, q="pool")

            for tt in range(12):
                pt, bpt = nps()
                for kc in range(8):
                    MM(pt[:], hT[:, kc, tt * 128:(tt + 1) * 128], w0[:, kc, :], kc == 0, kc == 7, [bw0, bh[tt // 4]], [bpt])
                COPY(utok[:, tt, :], pt[:], [bpt], [butok[tt // 4]], eng=("act" if tt % 2 else "dve"))
            w2_, bw2 = w2s, bw2s
            for tb in range(3):
                pts = []
                for c in range(3):
                    pt, bpt = nps()
                    for kc in range(8):
                        MM(pt[:], w1_[:, kc, c * 128:(c + 1) * 128], hT[:, kc, blk(tb)], kc == 0, kc == 7, [bw1, bh[tb]], [bpt])
                    ACT(sqt[:, c, :], pt[:], AF.Square, [bpt], [bsqt3[c]])
                    pts.append((pt, bpt))
                p2, bp2 = nps()
                for c in range(3):
                    MM(p2[:], ones[:], sqt[:, c, :], c == 0, c == 2, [bsqt3[c], bconst], [bp2])
                r, br = rstd_from(p2[:], bp2, 128, 512, 1.0 / 384)
                for c in range(3):
                    STT(qln[:, c, blk(tb)], pts[c][0][:], qag_sb[:, i, c:c + 1], r, ALU.mult, ALU.mult,
                        [pts[c][1], br, bsm], [bqln[tb]])
                pts = []
                for c in range(2):
                    pt, bpt = nps()
                    for kc in range(8):
                        lw_ = w1_[:, kc, 384:512] if c == 0 else w2_[:, kc, 0:128]
                        MM(pt[:], lw_, hT[:, kc, blk(tb)], kc == 0, kc == 7, [bw1, bw2, bh[tb]], [bpt])
                    ACT(sqt[:, c, :], pt[:], AF.Square, [bpt], [bsqt3[c]])
                    pts.append((pt, bpt))
                p2, bp2 = nps()
                for c in range(2):
                    MM(p2[:], ones[:], sqt[:, c, :], c == 0, c == 1, [bsqt3[c], bconst], [bp2])
                r, br = rstd_from(p2[:], bp2, 128, 512, 1.0 / 256)
                for c in range(2):
                    STT(ckn[:, c, 256 + tb * 512:256 + (tb + 1) * 512], pts[c][0][:], kvag_sb[:, i, c:c + 1], r,
                        ALU.mult, ALU.mult, [pts[c][1], br, bsm], [bckn[1 + tb]])
                pt, bpt = nps()
                for kc in range(8):
                    MM(pt[64:96, :], w2_[:, kc, 128:160], hT[:, kc, blk(tb)], kc == 0, kc == 7, [bw2, bh[tb]], [bpt])
                COPY(kpe[64:96, 256 + tb * 512:256 + (tb + 1) * 512], pt[64:96, :], [bpt], [bkpe[1 + tb]])
            for tt in range(4):
                pt, bpt = nps()
                for kc in range(8):
                    MM(pt[:, 0:128], hT[:, kc, tt * 128:(tt + 1) * 128], w1_[:, kc, 384:512], kc == 0, kc == 7,
                       [bw1, bh[0]], [bpt])
                for kc in range(8):
                    MM(pt[:, 128:288], hT[:, kc, tt * 128:(tt + 1) * 128], w2_[:, kc, 0:160], kc == 0, kc == 7,
                       [bw2, bh[0]], [bpt])
                o, bo = ost[tt % 2], bost[tt % 2]
                ACT(o[:, 0:256], pt[:, 0:256], AF.Square, [bpt], [bo])
                P.op("dve", lambda h, o=o: h.tensor_reduce(out=sst[:, 0:1], in_=o[:, 0:256], axis=mybir.AxisListType.X, op=ALU.add),
                     reads=[bo], writes=[bsst])
                ACT(sst[:, 0:1], sst[:, 0:1], AF.Ln, [bsst, bconst], [bsst], bias=eps_sb[:, :], scale=1.0 / 256)
                ACT(sst[:, 0:1], sst[:, 0:1], AF.Exp, [bsst], [bsst], scale=-0.5)
                STT(o[:, 0:256], pt[:, 0:256], sst[:, 0:1], kvg_sb[:, i, :], ALU.mult, ALU.mult, [bpt, bsst, bsm], [bo])
                COPY(o[:, 256:288], pt[:, 256:288], [bpt], [bo])
                STORE(ckv_out[i, tt * 128:(tt + 1) * 128, :], o[:, 0:256], bo)
                STORE(kpe_out[i, tt * 128:(tt + 1) * 128, :], o[:, 256:288], bo)

            s_ = wt_i[0] % NW
            wt_i[0] += 1
            wq = WT[s_][:].rearrange("p a b -> p (a b)")[:, 0:2304].rearrange("p (k n) -> p k n", k=3)
            bwC = bWT[s_]
            P.dma("pool", lambda h: h.dma_start(out=wq, in_=kview(wqb[i])), writes=[bwC], sbuf=bwC)
            seq_tiles = [(0, 2), (2, 2), (4, 8)]
            case_of = {}
            for (t0, n_) in seq_tiles:
                for k_ in range(n_):
                    case_of[t0 + k_] = (0 if k_ == 0 else (2 if k_ == n_ - 1 else 1), t0, t0 + n_)
            for g in range(4):
                for tb in range(3):
                    pt, bpt = nps()
                    for q_ in range(4):
                        tt = tb * 4 + q_
                        case, lo_, hi_ = case_of[tt]
                        rels = [r_ for r_ in (-1, 0, 1) if lo_ <= tt + r_ < hi_]
                        for n_, r_ in enumerate(rels):
                            MM(pt[:, q_ * 128:(q_ + 1) * 128], utok[:, tt + r_, g * 128:(g + 1) * 128],
                               PAs[:, case * 12 + g * 3 + (r_ + 1), :], n_ == 0, n_ == len(rels) - 1,
                               [butok[(tt + r_) // 4], bwB], [bpt])
                    COPY(diffT[:, g, blk(tb)], pt[:], [bpt], [bdiff[tb]], eng=("act" if (g + tb) % 2 else "dve"))
            for g in range(4):
                for tb in range(3):
                    pt, bpt = nps()
                    MM(pt[:], pw[:, g, :], diffT[:, g, blk(tb)], True, True, [bwB, bdiff[tb]], [bpt])
                    TS(poolT[:, g, blk(tb)], pt[:], pscale_sb[:, i, g:g + 1], None, ALU.mult, None, [bpt, bsm], [bpoolT[tb]])

            handover(phB, phC)
            wv = wkv[:].rearrange("p k (h e) -> p k h e", h=8)[:, :, :, 64:128]
            vsteps = []
            for vt in range(14):
                def vs_(vt=vt):
                    pt, bpt = nps()
                    c0 = vt * 128
                    for kc in range(2):
                        MM(pt[:].rearrange("p (h e) -> p h e", h=8), ckn[:, kc, c0:c0 + 128], wv[:, kc, :, :], kc == 0, kc == 1,
                           [bckn[(c0 + 256) // 512 if vt >= 2 else 0], bw_], [bpt])
                    COPY(Vall[:, vt, :], pt[:], [bpt], [bVall[(c0 + 256) // 512 if vt >= 2 else 0]],
                         eng=("act" if vt % 2 else "dve"))
                vsteps.append(vs_)
            for kb in range(4):
                c0, c1 = (0, 256) if kb == 0 else (256 + (kb - 1) * 512, 256 + kb * 512)
                ACT(sqK[64:96, c0:c1], kpe[64:96, c0:c1], AF.Square, [bkpe[kb]], [bsqK[kb]])

            batt = bh
            nsq = [0]

            def prod_steps(hh):
                Qh, bQh = QT[hh % 2], bQT[hh % 2]
                Kh, bKh = KT[hh % 2], bKT[hh % 2]
                Qs, Ks = [], []
                for tb in range(3):
                    box = {}

                    def s1(tb=tb, box=box):
                        pt, bpt = hps()
                        for kc in range(3):
                            MM(pt[0:96, :], wq[:, kc, hh * 96:(hh + 1) * 96], qln[:, kc, blk(tb)], kc == 0, kc == 2,
                               [bwC, bqln[tb]], [bpt])
                        k_ = nsq[0] % 3
                        nsq[0] += 1
                        box["fin"] = pnorm_start(pt, bpt, 96, 512, ones96, 1.0 / 96, gqm_sb[:, i:i + 1], Qh[:, blk(tb)],
                                                 [bQh], sqt[:, k_, :], bsqt3[k_])
                    rp = None
                    if tb >= 1:
                        rp = (lambda tb=tb: rope(Qh, bQh, tb * 512, (tb - 1) * 512, RM, RC, RS, rt, brt, [bw_, bwR]))
                    Qs.append((s1, (lambda box=box: box["fin"]()), rp))
                for kb in range(4):
                    c0, c1 = (0, 256) if kb == 0 else (256 + (kb - 1) * 512, 256 + kb * 512)
                    n_ = c1 - c0
                    box = {}

                    def k1(kb=kb, c0=c0, c1=c1, n_=n_, box=box):
                        pt, bpt = hps()
                        for kc in range(2):
                            MM(pt[0:64, 0:n_], wkv[:, kc, hh * 128:hh * 128 + 64], ckn[:, kc, c0:c1], kc == 0, kc == 1,
                               [bw_, bckn[kb]], [bpt])
                        ACT(sqK[0:64, c0:c1], pt[0:64, 0:n_], AF.Square, [bpt], [bsqK[kb]])
                        box["pt"] = (pt, bpt)

                    def k2(kb=kb, c0=c0, c1=c1, n_=n_, box=box):
                        pt, bpt = box["pt"]
                        p2, bp2 = nps()
                        MM(p2[0:96, 0:n_], ones96, sqK[:, c0:c1], True, True, [bsqK[kb], bconst], [bp2])
                        r, br = rstd_from(p2[0:96, 0:n_], bp2, 96, n_, 1.0 / 96)
                        STT(Kh[0:64, c0:c1], pt[0:64, 0:n_], gkm_sb[0:64, i:i + 1], r[0:64, :], ALU.mult, ALU.mult,
                            [bpt, br, bsm], [bKh])
                        STT(Kh[64:96, c0:c1], kpe[64:96, c0:c1], gkm_sb[64:96, i:i + 1], r[64:96, :], ALU.mult, ALU.mult,
                            [bkpe[kb], br, bsm], [bKh])
                    rp = None
                    if kb >= 2:
                        rp = (lambda kb=kb, c0=c0: rope(Kh, bKh, c0, (kb - 2) * 512, RM, RC, RS, rt, brt, [bw_, bwR]))
                    Ks.append((k1, k2, rp))
                (s1_0, fq_0, _), (s1_1, fq_1, rq_1), (s1_2, fq_2, rq_2) = Qs
                (k1_0, k2_0, _), (k1_1, k2_1, _), (k1_2, k2_2, rk_2), (k1_3, k2_3, rk_3) = Ks
                return [s1_0, k1_0, fq_0, k2_0,
                        s1_1, k1_1, fq_1, k2_1,
                        s1_2, k1_2, rq_1, fq_2, k2_2,
                        k1_3, rq_2, k2_3, rk_2, rk_3]

            def att_units(hh):
                j, b0 = hh // 2, (hh % 2) * 64
                Qh, bQh = QT[hh % 2], bQT[hh % 2]
                Kh, bKh = KT[hh % 2], bKT[hh % 2]
                units = []
                for s_ in range(2):
                    tiles = []
                    for c2 in range(2):
                        k0 = 256 + s_ * 256 + c2 * 128
                        tiles.append((Kh[:, k0:k0 + 128], Qh[:, s_ * 256:(s_ + 1) * 256], [bKh, bQh],
                                      Vall[:, 2 + s_ * 2 + c2, j * 128:(j + 1) * 128], [bVall[1]], None, None))
                    units.append(dict(tiles=tiles, b0=b0, out=attT[b0:b0 + 64, j, s_ * 256:(s_ + 1) * 256],
                                      outbuf=batt[0], ncq=256))
                for qh in range(2):
                    q0 = 512 + qh * 512
                    tiles = []
                    for c in range(10):
                        k0 = c * 128 if c < 2 else 768 + (c - 2) * 128
                        vt = c if c < 2 else 6 + (c - 2)
                        vb_ = bVall[0] if c < 2 else bVall[2 + (c - 2) // 4]
                        tiles.append((Kh[:, k0:k0 + 128], Qh[:, q0:q0 + 512], [bKh, bQh],
                                      Vall[:, vt, j * 128:(j + 1) * 128], [vb_], None, None))
                    units.append(dict(tiles=tiles, b0=b0, out=attT[b0:b0 + 64, j, q0:q0 + 512], outbuf=batt[1 + qh],
                                      ncq=512))
                return units

            pools["rr"] = [0, 1, 2, 7]
            for st_ in merged(vsteps, prod_steps(0)):
                st_()
            for hh in range(8):
                A = unit_steps(att_units(hh), PT, bPT)
                B = prod_steps(hh + 1) if hh + 1 < 8 else []
                for st_ in merged(A, B):
                    st_()
            pools["rr"] = [0, 1, 2, 3, 4, 5, 6]
            handover(phC, [bsq] + btmp)
            out_proj(l, owout[i], poolT, bpoolT, attT, batt, sq, bsq, tmp, btmp)
            return allb

        def rope(X, bX, c0, p0, RM, RC, RS, rt, brt, bw_):
            pt, bpt = nps()
            MM(pt[0:96, :], RM[:], X[:, c0:c0 + 512], True, True, [bX, bw_[0]], [bpt])
            TT(rt[0][64:96, :], X[64:96, c0:c0 + 512], RC[64:96, p0:p0 + 512], ALU.mult, [bX] + bw_[1:], [brt[0]])
            TT(rt[1][64:96, :], pt[64:96, :], RS[64:96, p0:p0 + 512], ALU.mult, [bpt] + bw_[1:], [brt[1]])
            TT(X[64:96, c0:c0 + 512], rt[0][64:96, :], rt[1][64:96, :], ALU.add, [brt[0], brt[1]], [bX])


        ada0A, ada0B = ada_steps(0, split=True)
        for st_ in ada0A:
            st_()
        for tb in (1, 2):
            P.dma("sp", lambda h, tb=tb: h.dma_start(out=yT[:, :, blk(tb)], in_=kview(xT)[:, :, blk(tb)]),
                  reads=[bWT[0], bWT[1]], writes=[by[tb]], sbuf=by[tb])
        prev = []
        for l in range(depth):
            if l % 2 == 0:
                prev = even_layer(l, prev, ada0B if l == 0 else [])
            else:
                prev = odd_layer(l, prev)
            prev = mlp(l, prev, l + 1 < depth)

        P.finish()
        P.emit()
    return nc


_NC_CACHE = {}


def _prep_shared(inp):
    f = lambda k: np.ascontiguousarray(np.asarray(inp[k], dtype=np.float32))
    sh = {}
    for k in ("ada_w", "mlp_w1", "mlp_w2", "even_w_in", "even_w_out", "odd_w_in", "odd_w_out", "w_q_b", "w_kv_b"):
        sh[k] = f(k)
    sh["ada_b"] = np.ascontiguousarray(f("ada_b").reshape(4, 48, 128).transpose(2, 0, 1)).reshape(128, 192)
    sh["n1g"] = np.ascontiguousarray(f("norm1_g").reshape(4, 8, 128).transpose(2, 0, 1)).reshape(128, 32)
    sh["n2g"] = np.ascontiguousarray(f("norm2_g").reshape(4, 8, 128).transpose(2, 0, 1)).reshape(128, 32)
    sh["convw"] = np.ascontiguousarray(f("even_conv_w").reshape(2, 3, 4, 128).transpose(3, 0, 1, 2)).reshape(128, 24)
    sh["gqna"] = np.ascontiguousarray(np.tile(f("na_q_norm").T, (2, 1)))
    sh["gkna"] = np.ascontiguousarray(np.tile(f("na_k_norm").T, (2, 1)))
    sh["gktok"] = np.ascontiguousarray(np.broadcast_to(f("na_k_norm")[None], (128, 2, 64))).reshape(128, 128)
    sh["rpbx"] = _rpb_expand(f("na_rpb"))
    sh["poolw"] = np.ascontiguousarray(f("pool_w").transpose(0, 2, 1, 3)).reshape(2, 128, 512)
    sh["pscale"] = np.ascontiguousarray(f("pool_scale").reshape(2, 4, 128).transpose(2, 0, 1)).reshape(128, 8)
    sh["qag"] = np.ascontiguousarray(f("q_a_norm").reshape(2, 3, 128).transpose(2, 0, 1)).reshape(128, 6)
    sh["kvag"] = np.ascontiguousarray(f("kv_a_norm").reshape(2, 2, 128).transpose(2, 0, 1)).reshape(128, 4)
    sh["kvgrep"] = np.ascontiguousarray(np.broadcast_to(f("kv_a_norm")[None], (128, 2, 256))).reshape(128, 512)
    sh["gqm"] = np.ascontiguousarray(f("mla_q_norm").T)
    sh["gkm"] = np.ascontiguousarray(f("mla_k_norm").T)
    sh.update(_const_tables())
    return sh


def _prep_core(inp, c):
    f = lambda k: np.asarray(inp[k], dtype=np.float32)
    m = {}
    xp = f("x_prompt")[2 * c:2 * c + 2].reshape(512, D)
    xs = f("x_sample")[c]
    m["xT"] = np.ascontiguousarray(np.concatenate([xp, xs], 0).T)
    cc = np.stack([f("c_ctx"), f("c")[c]], 1)
    m["cT"] = np.ascontiguousarray(cc.reshape(8, 128, 2).transpose(1, 0, 2)).reshape(128, 16)
    ck = f("cache_na_k")[c]
    kk = ck.transpose(0, 2, 3, 1).reshape(2, 4, 128, 256).transpose(0, 2, 1, 3)
    m["nakT"] = np.ascontiguousarray(kk).reshape(2, 128, 1024)
    m["nav"] = np.ascontiguousarray(f("cache_na_v")[c].reshape(2, 256, 512))
    m["ckvT"] = np.ascontiguousarray(f("cache_mla_ckv")[c].transpose(0, 2, 1))
    m["kpeT"] = np.ascontiguousarray(f("cache_mla_kpe")[c].transpose(0, 2, 1))
    return m


DEPTH = 4


def kernel(**inputs):
    depth = DEPTH
    if depth not in _NC_CACHE:
        _NC_CACHE[depth] = build(depth)
    nc = _NC_CACHE[depth]
    sh = _prep_shared(inputs)
    in_maps = []
    for c in range(NCORES):
        m = dict(sh)
        m.update(_prep_core(inputs, c))
        in_maps.append(m)
    res = run_bass_kernel_spmd(nc, in_maps, core_ids=list(range(NCORES)))
    R = res.results
    yp = np.zeros((16, 256, D), np.float32)
    ys = np.zeros((8, 1024, D), np.float32)
    nak = np.zeros((16, 2, 256, 8, 64), np.float32)
    nav = np.zeros((16, 2, 256, 8, 64), np.float32)
    ckv = np.zeros((16, 2, 256, 256), np.float32)
    kpe = np.zeros((16, 2, 256, 32), np.float32)
    for c in range(NCORES):
        y = np.asarray(R[c]["yT_out"]).T
        yp[2 * c:2 * c + 2] = y[0:512].reshape(2, 256, D)
        ys[c] = y[512:]
        for s in range(2):
            nak[2 * c + s] = np.asarray(R[c]["nak_out"])[:, s * 256:(s + 1) * 256].reshape(2, 256, 8, 64)
            nav[2 * c + s] = np.asarray(R[c]["nav_out"])[:, s * 256:(s + 1) * 256].reshape(2, 256, 8, 64)
            ckv[2 * c + s] = np.asarray(R[c]["ckv_out"])[:, s * 256:(s + 1) * 256]
            kpe[2 * c + s] = np.asarray(R[c]["kpe_out"])[:, s * 256:(s + 1) * 256]
    return (yp, ys, nak, nav, ckv, kpe)
```

```python
from contextlib import ExitStack
import numpy as np
import concourse.bass as bass
import concourse.mybir as mybir
from concourse.bass_utils import run_bass_kernel_spmd

F32 = mybir.dt.float32
BF16 = mybir.dt.bfloat16
AF = mybir.ActivationFunctionType
ALU = mybir.AluOpType

NCORES = 8
D = 1024
T = 1536
EPS = 1e-6


class Buf:
    __slots__ = ("name", "lw", "rd", "dsem", "dcount")

    def __init__(self, name):
        self.name = name
        self.lw = None
        self.rd = {}
        self.dsem = None
        self.dcount = 0


class Eng:
    def __init__(self, name, handle, sem, key):
        self.name, self.h, self.sem, self.key = name, handle, sem, key
        self.count = 0
        self.ops = []
        self.waited = {}


class Prog:
    EPOCH = 4096

    def __init__(self, nc, stack, same_engine_sync=True):
        self.nc, self.stack = nc, stack
        self.sems = {}
        self.same_engine_sync = same_engine_sync
        self.E = {}
        for name, h in (("pe", nc.tensor), ("act", nc.scalar), ("dve", nc.vector),
                        ("pool", nc.gpsimd), ("sp", nc.sync)):
            sem = stack.enter_context(nc.semaphore("sem_" + name))
            key = "E_" + name
            self.sems[key] = sem
            self.E[name] = Eng(name, h, sem, key)
        self.nbuf = 0
        self.out_events = []

    def buf(self, name=None):
        self.nbuf += 1
        return Buf(name or "b%d" % self.nbuf)

    def bufs(self, n, name="b"):
        return [self.buf("%s%d" % (name, i)) for i in range(n)]

    def _dsem(self, b):
        if b.dsem is None:
            key = "D%d" % len(self.sems)
            self.sems[key] = self.stack.enter_context(self.nc.semaphore(key))
            b.dsem = key
        return b.dsem

    def _deps(self, eng, reads, writes):
        need = {}

        def add(ev, kind):
            if ev is None:
                return
            key, val, ename = ev
            if ename == eng.name:
                if eng.name == "pe" or not self.same_engine_sync:
                    return
            if need.get(key, 0) < val:
                need[key] = val

        for b in reads:
            add(b.lw, "raw")
        for b in writes:
            add(b.lw, "waw")
            for key, (val, ename) in b.rd.items():
                add((key, val, ename), "war")
        waits = []
        for key, val in need.items():
            if eng.waited.get(key, 0) >= val:
                continue
            eng.waited[key] = val
            waits.append((key, val))
        return waits

    def _record(self, ev, reads, writes):
        key, val, ename = ev
        for b in reads:
            old = b.rd.get(key)
            if old is None or old[0] < val:
                b.rd[key] = (val, ename)
        for b in writes:
            b.lw = ev
            b.rd = {}

    def op(self, ename, fn, reads=(), writes=()):
        eng = self.E[ename]
        waits = self._deps(eng, reads, writes)
        ep, idx = divmod(eng.count, self.EPOCH)
        eng.count += 1
        key = "%s_%d" % (eng.key, ep)
        if key not in self.sems:
            self.sems[key] = self.stack.enter_context(self.nc.semaphore(key))
        ev = (key, idx + 1, eng.name)
        eng.ops.append((waits, fn, (key, 1)))
        self._record(ev, reads, writes)
        return ev

    def dma(self, qname, fn, reads=(), writes=(), sbuf=None, is_output=False, chain=False):
        eng = self.E[qname]
        if chain and sbuf.lw is not None and sbuf.lw[0] == sbuf.dsem and not sbuf.rd:
            saved = sbuf.lw
            sbuf.lw = None
            waits = self._deps(eng, reads, writes)
            sbuf.lw = saved
        else:
            waits = self._deps(eng, reads, writes)
        key = self._dsem(sbuf)
        sbuf.dcount += 16
        ev = (key, sbuf.dcount, "dma")
        eng.ops.append((waits, fn, (key, 16)))
        self._record(ev, reads, writes)
        if is_output:
            self.out_events.append(ev)
        return ev

    def finish(self):
        eng = self.E["sp"]
        need = {}
        for key, val, _ in self.out_events:
            if need.get(key, 0) < val:
                need[key] = val
        eng.ops.append((list(need.items()), None, None))

    def emit(self):
        nc, sems = self.nc, self.sems
        with nc.Block() as block:
            def mk(eng):
                def body(h):
                    for waits, fn, inc in eng.ops:
                        for key, val in waits:
                            h.wait_ge(sems[key], val)
                        if fn is None:
                            continue
                        ins = fn(h)
                        if inc is not None:
                            ins.then_inc(sems[inc[0]], inc[1])
                return body
            block.tensor(mk(self.E["pe"]))
            block.scalar(mk(self.E["act"]))
            block.vector(mk(self.E["dve"]))
            block.gpsimd(mk(self.E["pool"]))
            block.sync(mk(self.E["sp"]))


def _const_tables():
    kc = np.arange(64)[:, None]
    qc = np.arange(64)[None, :]
    qcs = np.clip(qc - 8, 0, 48)
    colvalid = ((kc >= qcs) & (kc < qcs + 16)).astype(np.float32)
    MF = np.zeros((128, 14, 64), np.float32)
    MI = np.zeros((128, 14, 64), np.float32)
    for a in range(2):
        for ei in range(14):
            e = 6 - ei
            if -7 <= e + a <= 7:
                MF[a * 64:(a + 1) * 64, ei, :] = colvalid
            if -4 <= e + a <= 3:
                MI[a * 64:(a + 1) * 64, ei, :] = colvalid
    PA = np.zeros((128, 3, 4, 3, 128), np.float32)
    L = 384
    for g, w in enumerate((2, 4, 8, 16)):
        A = np.zeros((L, L), np.float32)
        for t in range(L):
            lo = min(max(t - w // 2, 0), L)
            hi = min(max(t - w // 2 + w, 0), L)
            A[lo:hi, t] = 1.0 / float(hi - lo)
            A[t, t] -= 1.0
        for case, o in enumerate((0, 128, 256)):
            for rel in (-1, 0, 1):
                s = o + rel * 128
                if s < 0 or s + 128 > L:
                    continue
                PA[:, case, g, rel + 1, :] = A[s:s + 128, o:o + 128]
    n = 1024
    t = np.arange(n)
    row = (t // 64).astype(np.float64)
    col = (t % 64).astype(np.float64)
    inv = 1.0 / (10000.0 ** (np.arange(0, 16, 2, dtype=np.float64) / 16.0))
    ang = np.concatenate([row[:, None] * inv, col[:, None] * inv], -1)
    cos, sin = np.cos(ang).astype(np.float32), np.sin(ang).astype(np.float32)
    RC = np.zeros((96, n), np.float32)
    RS = np.zeros((96, n), np.float32)
    RC[64:80] = cos.T
    RC[80:96] = cos.T
    RS[64:80] = sin.T
    RS[80:96] = sin.T
    RM = np.zeros((96, 96), np.float32)
    for r in range(16):
        RM[80 + r, 64 + r] = -1.0
        RM[64 + r, 80 + r] = 1.0
    return dict(MF=MF.reshape(128, 896), MI=MI.reshape(128, 896), PA=PA.reshape(128, 4608),
                RC=RC, RS=RS, RM=RM)


def _rpb_expand(rpb):
    a = np.arange(128)[:, None, None] // 64
    kc = np.arange(128)[:, None, None] % 64
    e = 6 - np.arange(14)[None, :, None]
    qc = np.arange(64)[None, None, :]
    dr = np.clip(e + a + 7, 0, 14) + 0 * qc
    dc = np.clip(kc - qc + 15, 0, 30) + 0 * e
    out = rpb[:, :, dr, dc]
    return np.ascontiguousarray(out.transpose(0, 2, 1, 3, 4)).reshape(2, 128, 8 * 896)


def _fm(v, n):
    lead = v.shape[:-1]
    r = v.reshape(lead + (n, 128))
    return np.ascontiguousarray(np.moveaxis(r, -1, 0))


def build(depth=4):
    nc = bass.Bass("TRN2", target_bir_lowering=False)
    dram = {}

    def din(name, shape):
        dram[name] = nc.dram_tensor(name, list(shape), F32, kind="ExternalInput").ap()
        return dram[name]

    def dout(name, shape):
        dram[name] = nc.dram_tensor(name, list(shape), F32, kind="ExternalOutput").ap()
        return dram[name]

    xT = din("xT", (D, T))
    cT = din("cT", (128, 16))
    ada_w = din("ada_w", (4, D, 6144))
    ada_b = din("ada_b", (128, 4 * 48))
    n1g = din("n1g", (128, 32))
    n2g = din("n2g", (128, 32))
    w1 = din("mlp_w1", (4, D, 4096))
    w2 = din("mlp_w2", (4, 4096, D))
    ewin = din("even_w_in", (2, D, 3072))
    ewout = din("even_w_out", (2, D, D))
    owin = din("odd_w_in", (2, D, 1184))
    owout = din("odd_w_out", (2, D, D))
    convw = din("convw", (128, 24))
    gqna = din("gqna", (128, 2))
    gkna = din("gkna", (128, 2))
    gktok = din("gktok", (128, 128))
    rpbx = din("rpbx", (2, 128, 8 * 896))
    MFd = din("MF", (128, 896))
    MId = din("MI", (128, 896))
    PAd = din("PA", (128, 4608))
    RCd = din("RC", (96, 1024))
    RSd = din("RS", (96, 1024))
    RMd = din("RM", (96, 96))
    poolw = din("poolw", (2, 128, 512))
    pscale = din("pscale", (128, 8))
    qag = din("qag", (128, 6))
    kvag = din("kvag", (128, 4))
    kvgrep = din("kvgrep", (128, 512))
    wqb = din("w_q_b", (2, 384, 768))
    wkvb = din("w_kv_b", (2, 256, 1024))
    gqm = din("gqm", (96, 2))
    gkm = din("gkm", (96, 2))
    nakT = din("nakT", (2, 128, 1024))
    nav = din("nav", (2, 256, 512))
    ckvT = din("ckvT", (2, 256, 256))
    kpeT = din("kpeT", (2, 32, 256))

    yT_out = dout("yT_out", (D, T))
    nak_out = dout("nak_out", (2, 512, 512))
    nav_out = dout("nav_out", (2, 512, 512))
    ckv_out = dout("ckv_out", (2, 512, 256))
    kpe_out = dout("kpe_out", (2, 512, 32))

    with ExitStack() as st:
        P = Prog(nc, st)

        def sb(name, shape, dt=F32):
            return st.enter_context(nc.sbuf_tensor(name, list(shape), dt))

        yT = sb("yT", (128, 8, T))
        hT = sb("hT", (128, 8, T), BF16)
        NW = 2
        WT = [sb("WT%d" % i, (128, 8, 512), BF16) for i in range(NW)]
        bWT = P.bufs(NW, "WT")
        mod = sb("mod", (128, 4, 48, 2))
        A1 = sb("A1", (128, 4, 8, 2))
        A2 = sb("A2", (128, 4, 8, 2))
        ones = sb("ones", (128, 128), BF16)
        bdiag = sb("bdiag", (128, 128), BF16)
        sTb = sb("sTb", (128, 8, 2), BF16)
        c_sb = sb("c_sb", (128, 16))
        adab_sb = sb("adab_sb", (128, 4, 48))
        n1g_sb = sb("n1g_sb", (128, 4, 8))
        n2g_sb = sb("n2g_sb", (128, 4, 8))
        convw_sb = sb("convw_sb", (128, 2, 3, 4))
        gqna_sb = sb("gqna_sb", (128, 2))
        gkna_sb = sb("gkna_sb", (128, 2))
        gktok_sb = sb("gktok_sb", (128, 2, 64))
        pscale_sb = sb("pscale_sb", (128, 2, 4))
        qag_sb = sb("qag_sb", (128, 2, 3))
        kvag_sb = sb("kvag_sb", (128, 2, 2))
        gqm_sb = sb("gqm_sb", (96, 2))
        gkm_sb = sb("gkm_sb", (96, 2))
        eps_sb = sb("eps_sb", (128, 1))
        kvg_sb = sb("kvg_sb", (128, 2, 256))
        rstd = [sb("rstd%d" % i, (128, 512)) for i in range(2)]
        brstd = P.bufs(2, "rstd")
        ARENA_W = 28000
        arena = sb("arena", (128, ARENA_W))
        dummy = sb("gdummy", (128, 2))

        class Arena:
            def __init__(self):
                self.off = 0

            def take(self, shape, dt=F32):
                n = 1
                for s in shape[1:]:
                    n *= s
                words = n if dt == F32 else (n + 1) // 2
                a = arena[0:shape[0], self.off:self.off + words]
                self.off += words
                assert self.off <= ARENA_W, ("arena overflow", self.off)
                if dt != F32:
                    a = a.bitcast(dt)
                if len(shape) == 3:
                    a = a.rearrange("p (a b) -> p a b", a=shape[1])
                elif len(shape) == 4:
                    a = a.rearrange("p (a b c) -> p a b c", a=shape[1], b=shape[2])
                return a

        ps = [st.enter_context(nc.psum_tensor("ps%d" % i, [128, 512], F32)) for i in range(8)]
        bps = P.bufs(8, "ps")
        psn = [0]

        pools = {"rr": [0, 1, 2, 3, 4, 5, 6], "prod": [5, 6], "att": [3, 4]}
        pcnt = {"rr": 0, "prod": 0, "att": 0}

        def _take(pool):
            lst = pools[pool]
            i = lst[pcnt[pool] % len(lst)]
            pcnt[pool] += 1
            return ps[i], bps[i]

        def nps():
            return _take("rr")

        def hps():
            return _take("prod")

        def aps():
            return _take("att")

        def handover(prev_bufs, new_bufs):
            g = P.buf("guard")
            P.op("dve", lambda h: h.memset(dummy[:], 0.0), writes=list(prev_bufs) + [g])
            for b in new_bufs:
                b.lw = g.lw
                b.rd = {}

        by = P.bufs(3, "y")
        bh = P.bufs(3, "h")
        bconst = P.buf("const")
        bmodA = P.bufs(4, "modA")
        bmodB = P.bufs(4, "modB")

        def blk(tb):
            return slice(tb * 512, (tb + 1) * 512)

        def mcol(tb):
            return 0 if tb == 0 else 1

        def MM(out, lhsT, rhs, start, stop, reads, writes):
            P.op("pe", lambda h: h.matmul(out, lhsT=lhsT, rhs=rhs, start=start, stop=stop),
                 reads=reads, writes=writes)

        def ACT(out, in_, func, reads, writes, bias=None, scale=None):
            kw = {}
            if bias is not None:
                kw["bias"] = bias
            if scale is not None:
                kw["scale"] = scale
            P.op("act", lambda h: h.activation(out=out, in_=in_, func=func, **kw), reads=reads, writes=writes)

        def TT(out, in0, in1, op, reads, writes, eng="dve"):
            P.op(eng, lambda h: h.tensor_tensor(out=out, in0=in0, in1=in1, op=op), reads=reads, writes=writes)

        def TS(out, in0, s1, s2, op0, op1, reads, writes, eng="dve"):
            if op1 is None:
                P.op(eng, lambda h: h.tensor_scalar(out=out, in0=in0, scalar1=s1, scalar2=None, op0=op0),
                     reads=reads, writes=writes)
            else:
                P.op(eng, lambda h: h.tensor_scalar(out=out, in0=in0, scalar1=s1, scalar2=s2, op0=op0, op1=op1),
                     reads=reads, writes=writes)

        def STT(out, in0, scalar, in1, op0, op1, reads, writes):
            P.op("dve", lambda h: h.scalar_tensor_tensor(out=out, in0=in0, scalar=scalar, in1=in1, op0=op0, op1=op1),
                 reads=reads, writes=writes)

        def RECIP(out, in_, reads, writes):
            P.op("dve", lambda h: h.reciprocal(out=out, in_=in_), reads=reads, writes=writes)

        def COPY(out, in_, reads, writes, eng="dve"):
            if eng == "act":
                ACT(out, in_, AF.Copy, reads, writes)
            else:
                P.op(eng, lambda h: h.tensor_copy(out=out, in_=in_), reads=reads, writes=writes)

        def LOAD(out, in_, b, q="sp", chain=False):
            P.dma(q, lambda h: h.dma_start(out=out, in_=in_), writes=[b], sbuf=b, chain=chain)

        def STORE(out, in_, b):
            P.dma("sp", lambda h: h.dma_start(out=out, in_=in_), reads=[b], sbuf=b, is_output=True)

        wt_i = [0]

        def wtile(srcs):
            s = wt_i[0] % NW
            wt_i[0] += 1
            for (c0, ncol, src) in srcs:
                nkc = src.shape[1]
                P.dma("pool", lambda h, s=s, c0=c0, ncol=ncol, src=src, nkc=nkc:
                      h.dma_start(out=WT[s][:, 0:nkc, c0:c0 + ncol], in_=src),
                      writes=[bWT[s]], sbuf=bWT[s])
            return WT[s], bWT[s]

        def kview(w2d):
            return w2d.rearrange("(kc k) n -> k kc n", k=128)

        rs_i = [0]

        def rstd_from(ss_ps, bss, npart, ncol, inv_n):
            i = rs_i[0] % len(rstd)
            rs_i[0] += 1
            r, br = rstd[i], brstd[i]
            ACT(r[0:npart, 0:ncol], ss_ps, AF.Ln, [bss, bconst], [br], bias=eps_sb[0:npart, :], scale=inv_n)
            ACT(r[0:npart, 0:ncol], r[0:npart, 0:ncol], AF.Exp, [br], [br], scale=-0.5)
            return r[0:npart, 0:ncol], br

        bsm = P.buf("small")
        LOAD(c_sb[:], cT, bsm, chain=True)
        LOAD(adab_sb[:], ada_b.rearrange("p (l n) -> p l n", l=4), bsm, chain=True)
        LOAD(n1g_sb[:], n1g.rearrange("p (l n) -> p l n", l=4), bsm, chain=True)
        LOAD(n2g_sb[:], n2g.rearrange("p (l n) -> p l n", l=4), bsm, chain=True)
        LOAD(convw_sb[:], convw.rearrange("p (i t c) -> p i t c", i=2, t=3), bsm, chain=True)
        LOAD(gqna_sb[:], gqna, bsm, chain=True)
        LOAD(gkna_sb[:], gkna, bsm, chain=True)
        LOAD(gktok_sb[:], gktok.rearrange("p (i d) -> p i d", i=2), bsm, chain=True)
        LOAD(pscale_sb[:], pscale.rearrange("p (i c) -> p i c", i=2), bsm, chain=True)
        LOAD(qag_sb[:], qag.rearrange("p (i c) -> p i c", i=2), bsm, chain=True)
        LOAD(kvag_sb[:], kvag.rearrange("p (i c) -> p i c", i=2), bsm, chain=True)
        LOAD(gqm_sb[:], gqm, bsm, chain=True)
        LOAD(gkm_sb[:], gkm, bsm, chain=True)
        LOAD(kvg_sb[:], kvgrep.rearrange("p (i d) -> p i d", i=2), bsm, chain=True)
        LOAD(yT[:, :, blk(0)], kview(xT)[:, :, blk(0)], by[0])
        P.op("dve", lambda h: h.memset(ones[:], 1.0), writes=[bconst])
        P.op("dve", lambda h: h.memset(bdiag[:], 0.0), writes=[bconst])
        P.op("dve", lambda h: h.memset(bdiag[0:64, 0:64], 1.0), writes=[bconst])
        P.op("dve", lambda h: h.memset(bdiag[64:128, 64:128], 1.0), writes=[bconst])
        P.op("dve", lambda h: h.memset(eps_sb[:], EPS), writes=[bconst])
        TS(gqna_sb[:], gqna_sb[:], 0.125, None, ALU.mult, None, [bsm], [bsm])
        TS(gqm_sb[:], gqm_sb[:], float(96 ** -0.5), None, ALU.mult, None, [bsm], [bsm])
        ACT(sTb[:].rearrange("p k m -> p (k m)"), c_sb[:], AF.Silu, [bsm], [bsm])

        def ada_steps(l, slots=None, split=False):
            ptB, bptB = ps[7], bps[7]
            ptA, bptA = (ps[6], bps[6]) if split else (ptB, bptB)
            stepsA, stepsB = [], []
            for nt in range(12):
                def st_(nt=nt):
                    pt, bpt = (ptA, bptA) if nt < 4 else (ptB, bptB)
                    if slots is None:
                        w, bw = wtile([(0, 512, kview(ada_w[l])[:, :, nt * 512:(nt + 1) * 512])])
                    else:
                        w, bw = slots[nt % len(slots)]
                        P.dma("pool", lambda h, w=w, nt=nt: h.dma_start(out=w[:], in_=kview(ada_w[l])[:, :, nt * 512:(nt + 1) * 512]),
                              writes=[bw], sbuf=bw)
                    for j in range(4):
                        n = nt * 4 + j
                        for kc in range(8):
                            MM(pt[:, 2 * n:2 * n + 2], w[:, kc, j * 128:(j + 1) * 128], sTb[:, kc, :],
                               kc == 0, kc == 7, [bw, bsm], [bpt])
                (stepsA if nt < 4 else stepsB).append(st_)

            def finA():
                pv = ptA[:, 0:32].rearrange("p (n m) -> p n m", m=2)
                for m in range(2):
                    TT(mod[:, l, 0:16, m], pv[:, :, m], adab_sb[:, l, 0:16], ALU.add, [bptA, bsm], [bmodA[l]])
                for m in range(2):
                    STT(A1[:, l, :, m], mod[:, l, 8:16, m], 1.0, n1g_sb[:, l, :], ALU.add, ALU.mult,
                        [bmodA[l], bsm], [bmodA[l]])

            def finB():
                pv = ptB[:, 32:96].rearrange("p (n m) -> p n m", m=2)
                for m in range(2):
                    TT(mod[:, l, 16:48, m], pv[:, :, m], adab_sb[:, l, 16:48], ALU.add, [bptB, bsm], [bmodB[l]])
                for m in range(2):
                    STT(A2[:, l, :, m], mod[:, l, 32:40, m], 1.0, n2g_sb[:, l, :], ALU.add, ALU.mult,
                        [bmodB[l], bsm], [bmodB[l]])
            if split:
                return stepsA + [finA], stepsB + [finB]
            return stepsA + stepsB + [finA, finB]

        def norm_parts(l, which, tb, sq, bsq, tmp, btmp):
            Asc = A1 if which == 1 else A2
            sh0 = 0 if which == 1 else 24
            m = mcol(tb)
            bmw = bmodA[l] if which == 1 else bmodB[l]

            def pa():
                ACT(sq[:], yT[:, :, blk(tb)], AF.Square, [by[tb]], [bsq])

            def pb():
                pt, bpt = nps()
                for kc in range(8):
                    MM(pt[:], ones[:], sq[:, kc, :], kc == 0, kc == 7, [bsq, bconst], [bpt])
                r, br = rstd_from(pt[:], bpt, 128, 512, 1.0 / D)
                for kc in range(8):
                    STT(tmp[:, kc % 2, :], yT[:, kc, blk(tb)], Asc[:, l, kc, m:m + 1], r, ALU.mult, ALU.mult,
                        [by[tb], bmw, br], [btmp[kc % 2]])
                    ACT(hT[:, kc, blk(tb)], tmp[:, kc % 2, :], AF.Identity, [btmp[kc % 2], bmw], [bh[tb]],
                        bias=mod[:, l, sh0 + kc, m:m + 1])
            return pa, pb

        def norm_mod(l, which, sq, bsq, tmp, btmp):
            for tb in range(3):
                pa, pb = norm_parts(l, which, tb, sq, bsq, tmp, btmp)
                pa()
                pb()

        def pnorm_start(pt, bpt, npart, ncol, onesm, inv_n, gain, out, outbufs, sqt, bsqt, extra_reads=(), post=None):
            ACT(sqt[0:npart, 0:ncol], pt[0:npart, 0:ncol], AF.Square, [bpt], [bsqt])

            def fin():
                p2, bp2 = nps()
                MM(p2[0:npart, 0:ncol], onesm, sqt[0:npart, 0:ncol], True, True, [bsqt, bconst], [bp2])
                r, br = rstd_from(p2[0:npart, 0:ncol], bp2, npart, ncol, inv_n)
                if isinstance(out, list):
                    for (p0, p1, o_) in out:
                        STT(o_, pt[p0:p1, 0:ncol], gain[p0:p1, :], r[p0:p1, :], ALU.mult, ALU.mult,
                            [bpt, br, bsm] + list(extra_reads), outbufs)
                else:
                    STT(out, pt[0:npart, 0:ncol], gain, r, ALU.mult, ALU.mult, [bpt, br, bsm] + list(extra_reads), outbufs)
                if post is not None:
                    post()
            return fin

        def merge(A, B):
            ia = ib = 0
            while ia < len(A) or ib < len(B):
                if ia < len(A) and (ib >= len(B) or ia * len(B) <= ib * len(A)):
                    A[ia]()
                    ia += 1
                else:
                    B[ib]()
                    ib += 1

        def mlp(l, prev_bufs, with_ada):
            ar = Arena()
            pools["rr"] = [0, 1, 2, 3, 4, 5, 6] + ([] if with_ada else [7])
            hid = [ar.take((128, 8, T), BF16) for _ in range(2)]
            msk = [ar.take((128, 512)) for _ in range(2)]
            AW = [ar.take((128, 8, 512), BF16) for _ in range(2)]
            sqn = ar.take((128, 8, 512), BF16)
            tmpn = ar.take((128, 2, 512))
            bhid = [P.bufs(3) for _ in range(2)]
            bmsk = P.bufs(2)
            bAW = P.bufs(2, "AW")
            bsqn, btmpn = P.buf(), P.bufs(2)
            allb = sum(bhid, []) + bmsk + bAW + [bsqn] + btmpn
            handover(prev_bufs, allb)
            extra = ada_steps(l + 1, list(zip(AW, bAW))) if with_ada else []
            k = 0
            for qd in range(4):
                hq, bhq = hid[qd % 2], bhid[qd % 2]
                for half in range(2):
                    c0 = qd * 1024 + half * 512
                    w, bw = wtile([(0, 512, kview(w1[l])[:, :, c0:c0 + 512])])
                    for j in range(4):
                        jj = half * 4 + j
                        for tb in range(3):
                            pt, bpt = nps()
                            for kc in range(8):
                                MM(pt[:], w[:, kc, j * 128:(j + 1) * 128], hT[:, kc, blk(tb)], kc == 0, kc == 7,
                                   [bw, bh[tb]], [bpt])
                            mi = k % 2
                            k += 1
                            ACT(msk[mi][:], pt[:], AF.Square, [bpt], [bmsk[mi]])
                            STT(hq[:, jj, blk(tb)], pt[:], 0.0, msk[mi][:], ALU.is_gt, ALU.mult,
                                [bpt, bmsk[mi]], [bhq[tb]])
                    if extra:
                        extra.pop(0)()
                if qd == 3:
                    while extra:
                        extra.pop(0)()
                    ws = [wtile([(0, 512, kview(w2[l][qd * 1024:(qd + 1) * 1024, :])[:, :, half * 512:(half + 1) * 512])])
                          for half in range(2)]
                    pend = None
                    for tb in range(3):
                        m = mcol(tb)
                        for half in range(2):
                            w, bw = ws[half]
                            for j in range(4):
                                n = half * 4 + j
                                pt, bpt = nps()
                                for kc in range(8):
                                    MM(pt[:], w[:, kc, j * 128:(j + 1) * 128], hq[:, kc, blk(tb)], kc == 0, kc == 7,
                                       [bw, bhq[tb]], [bpt])
                                STT(yT[:, n, blk(tb)], pt[:], mod[:, l, 40 + n, m:m + 1], yT[:, n, blk(tb)],
                                    ALU.mult, ALU.add, [bpt, bmodB[l], by[tb]], [by[tb]])
                        if with_ada:
                            pa, pb = norm_parts(l + 1, 1, tb, sqn, bsqn, tmpn, btmpn)
                            if pend is not None:
                                pend()
                            pa()
                            pend = pb
                        else:
                            P.dma("sp", lambda h, tb=tb: h.dma_start(out=kview(yT_out)[:, :, blk(tb)], in_=yT[:, :, blk(tb)]),
                                  reads=[by[tb]], sbuf=by[tb], is_output=True)
                    if pend is not None:
                        pend()
                    continue
                for half in range(2):
                    w, bw = wtile([(0, 512, kview(w2[l][qd * 1024:(qd + 1) * 1024, :])[:, :, half * 512:(half + 1) * 512])])
                    for j in range(4):
                        n = half * 4 + j
                        for tb in range(3):
                            m = mcol(tb)
                            pt, bpt = nps()
                            for kc in range(8):
                                MM(pt[:], w[:, kc, j * 128:(j + 1) * 128], hq[:, kc, blk(tb)], kc == 0, kc == 7,
                                   [bw, bhq[tb]], [bpt])
                            STT(yT[:, n, blk(tb)], pt[:], mod[:, l, 40 + n, m:m + 1], yT[:, n, blk(tb)],
                                ALU.mult, ALU.add, [bpt, bmodB[l], by[tb]], [by[tb]])
                    if extra and qd < 3:
                        extra.pop(0)()
            while extra:
                extra.pop(0)()
            return allb

        def out_proj(l, wsrc, lo, blo, hi, bhi, sq, bsq, tmp, btmp):
            ws = [wtile([(0, 512, kview(wsrc)[:, :, nt * 512:(nt + 1) * 512])]) for nt in range(2)]
            pend = None
            for tb in range(3):
                m = mcol(tb)
                for nt in range(2):
                    w, bw = ws[nt]
                    for j in range(4):
                        n = nt * 4 + j
                        pt, bpt = nps()
                        for kc in range(8):
                            src = lo[:, kc, blk(tb)] if kc < 4 else hi[:, kc - 4, blk(tb)]
                            MM(pt[:], w[:, kc, j * 128:(j + 1) * 128], src, kc == 0, kc == 7,
                               [bw, blo[tb], bhi[tb]], [bpt])
                        STT(yT[:, n, blk(tb)], pt[:], mod[:, l, 16 + n, m:m + 1], yT[:, n, blk(tb)],
                            ALU.mult, ALU.add, [bpt, bmodB[l], by[tb]], [by[tb]])
                pa, pb = norm_parts(l, 2, tb, sq, bsq, tmp, btmp)
                if pend is not None:
                    pend()
                pa()
                pend = pb
            pend()

        def score_steps(u, PT, bPT):
            tiles, ncq = u["tiles"], u["ncq"]
            nt_ = len(tiles)
            per = 512 // ncq
            steps = []
            for i in range(0, nt_, per):
                def st_(i=i):
                    if i == 0 and u.get("pre") is not None:
                        u["pre"]()
                    grp = tiles[i:i + per]
                    pt, bpt = nps()
                    for g, tl in enumerate(grp):
                        MM(pt[:, g * ncq:(g + 1) * ncq], tl[0], tl[1], True, True, tl[2], [bpt])
                    w = len(grp) * ncq
                    ACT(PT[:, i * ncq:i * ncq + w], pt[:, 0:w], AF.Exp, [bpt], [bPT[i + g] for g in range(len(grp))])
                    for g, tl in enumerate(grp):
                        if tl[5] is not None:
                            base = (i + g) * ncq
                            for (c0_, n_, tab) in tl[5]:
                                dst = PT[:, base + c0_:base + c0_ + n_]
                                if tab is None:
                                    P.op("dve", lambda h, dst=dst: h.memset(dst, 0.0), reads=[], writes=[bPT[i + g]])
                                else:
                                    TT(dst, dst, tab, ALU.mult, [bPT[i + g]] + tl[6], [bPT[i + g]])
                steps.append(st_)
            return steps

        def finish_steps(u, PT, bPT):
            tiles, ncq, b0 = u["tiles"], u["ncq"], u["b0"]
            nt_ = len(tiles)
            box = {}
            steps = []
            for i in range(nt_):
                def st_(i=i):
                    if i == 0:
                        box["po"] = aps()
                        box["pd"] = aps()
                    (po, bpo), (pd, bpd) = box["po"], box["pd"]
                    tl = tiles[i]
                    MM(po[:, 0:ncq], tl[3], PT[:, i * ncq:(i + 1) * ncq], i == 0, i == nt_ - 1, [bPT[i]] + tl[4], [bpo])
                    MM(pd[:, 0:ncq], ones[:, :], PT[:, i * ncq:(i + 1) * ncq], i == 0, i == nt_ - 1, [bPT[i], bconst], [bpd])
                    if i == nt_ - 1:
                        ri = rs_i[0] % len(rstd)
                        rs_i[0] += 1
                        r, br = rstd[ri], brstd[ri]
                        ACT(r[b0:b0 + 64, 0:ncq], pd[b0:b0 + 64, 0:ncq], AF.Ln, [bpd], [br])
                        ACT(r[b0:b0 + 64, 0:ncq], r[b0:b0 + 64, 0:ncq], AF.Exp, [br], [br], scale=-1.0)
                        TT(u["out"], po[b0:b0 + 64, 0:ncq], r[b0:b0 + 64, 0:ncq], ALU.mult, [bpo, br], [u["outbuf"]])
                steps.append(st_)
            return steps

        upi = [0]

        def merged(A, B):
            out = []
            ia = ib = 0
            while ia < len(A) or ib < len(B):
                if ia < len(A) and (ib >= len(B) or ia * len(B) <= ib * len(A)):
                    out.append(A[ia])
                    ia += 1
                else:
                    out.append(B[ib])
                    ib += 1
            return out

        def unit_steps(units, PTs, bPTs):
            steps = []
            pend = []
            for u in units:
                k = upi[0]
                upi[0] += 1
                pt_, bpt_ = PTs[k % len(PTs)], bPTs[k % len(PTs)]
                steps += merged(score_steps(u, pt_, bpt_), pend)
                pend = finish_steps(u, pt_, bpt_)
            steps += pend
            return steps

        def even_layer(l, prev_bufs, extra_steps):
            i = l // 2
            x7 = [7] if l >= 1 else []
            pools["rr"] = [0, 1, 2, 3, 4, 5, 6] + x7
            ar = Arena()
            aT = ar.take((128, 4, T), BF16)
            qTm = ar.take((128, 8, T), BF16)
            kT = ar.take((128, 4, T), BF16)
            kcT = ar.take((128, 4, 256), BF16)
            V = ar.take((128, 14, 512), BF16)
            sqt2 = [ar.take((128, 512), BF16) for _ in range(2)]
            MFs = ar.take((128, 896), BF16)
            MIs = ar.take((128, 896), BF16)
            ost = [ar.take((128, 512)) for _ in range(2)]
            sst = ar.take((128, 8))
            rx = [ar.take((128, 512)) for _ in range(2)]
            u0 = ar.off
            sq = ar.take((128, 8, 512), BF16)
            tmp = ar.take((128, 2, 512))
            cgb = ar.take((128, T))
            ub = ar.take((128, T))
            ar.off = u0
            PT = [ar.take((128, 4096), BF16) for _ in range(2)]
            stg = [ar.take((128, 896)) for _ in range(2)]
            Gf = [ar.take((128, 896), BF16) for _ in range(2)]
            Gi = [ar.take((128, 896), BF16) for _ in range(2)]
            attT = hT[:, 0:4, :]
            bsq, btmp = P.buf(), P.bufs(2)
            baT, bq, bk = P.bufs(3, "aT"), P.bufs(3, "q"), P.bufs(3, "k")
            bkc, bV, bVc = P.buf("kc"), P.bufs(3, "V"), P.buf("Vc")
            bcg, bub, bsqt2 = P.buf(), P.buf(), P.bufs(2)
            bPT, bstg, bG = [P.bufs(8, "PTa"), P.bufs(8, "PTb")], P.bufs(2), P.bufs(2)
            bM, bost, bsst = P.buf(), P.bufs(2), P.buf()
            brx = P.bufs(2, "rx")
            phA = [bsq] + btmp + [bcg, bub]
            phC = bPT[0] + bPT[1] + bstg + bG
            allb = phA + baT + bq + bk + [bkc] + bV + [bVc, bM] + bsqt2 + bost + [bsst] + brx + phC
            handover(prev_bufs, [b for b in allb if b not in phC])
            rstd.extend(rx)
            brstd.extend(brx)


            if l == 0:
                norm_mod(l, 1, sq, bsq, tmp, btmp)

            W = kview(ewin[i])
            cw = convw_sb
            segs = [(0, 256), (256, 512), (512, 1536)]
            for j in range(4):
                w, bw = wtile([(0, 128, W[:, :, j * 128:(j + 1) * 128]),
                               (128, 128, W[:, :, 512 + j * 128:512 + (j + 1) * 128]),
                               (256, 128, W[:, :, 1024 + j * 128:1024 + (j + 1) * 128])])
                for tb in range(3):
                    pt, bpt = nps()
                    for kc in range(8):
                        MM(pt[:], w[:, kc, 128:256], hT[:, kc, blk(tb)], kc == 0, kc == 7, [bw, bh[tb]], [bpt])
                    COPY(cgb[:, blk(tb)], pt[:], [bpt], [bcg], eng="act")
                for tb in range(3):
                    pt, bpt = nps()
                    for kc in range(8):
                        MM(pt[:], w[:, kc, 256:384], hT[:, kc, blk(tb)], kc == 0, kc == 7, [bw, bh[tb]], [bpt])
                    TT(ub[:, blk(tb)], pt[:], cgb[:, blk(tb)], ALU.mult, [bpt, bcg], [bub])
                ACT(cgb[:], ub[:], AF.Identity, [bub, bsm], [bcg], scale=cw[:, i, 1, j:j + 1])
                for (s0, s1) in segs:
                    STT(cgb[:, s0 + 1:s1], ub[:, s0:s1 - 1], cw[:, i, 0, j:j + 1], cgb[:, s0 + 1:s1],
                        ALU.mult, ALU.add, [bub, bcg, bsm], [bcg])
                    STT(cgb[:, s0:s1 - 1], ub[:, s0 + 1:s1], cw[:, i, 2, j:j + 1], cgb[:, s0:s1 - 1],
                        ALU.mult, ALU.add, [bub, bcg, bsm], [bcg])
                for tb in range(3):
                    pt, bpt = nps()
                    for kc in range(8):
                        MM(pt[:], w[:, kc, 0:128], hT[:, kc, blk(tb)], kc == 0, kc == 7, [bw, bh[tb]], [bpt])
                    TT(aT[:, j, blk(tb)], pt[:], cgb[:, blk(tb)], ALU.mult, [bpt, bcg], [baT[tb]])

            LOAD(kcT[:], nakT[i].rearrange("p (c t) -> p c t", c=4), bkc, q="pool")
            LOAD(V[:, 12:14, :], nav[i].rearrange("(c p) n -> p c n", p=128), bVc, q="pool")
            LOAD(MFs[:], MFd, bM, q="pool")
            LOAD(MIs[:], MId, bM, q="pool")
            for which, dst, bdst, gain in ((3, None, bq, gqna_sb), (4, kT, bk, gkna_sb)):
                w, bw = wtile([(0, 512, W[:, :, which * 512:(which + 1) * 512])])
                pend = None
                kk = 0
                for j in range(4):
                    for tb in range(3):
                        pt, bpt = nps()
                        for kc in range(8):
                            MM(pt[:], w[:, kc, j * 128:(j + 1) * 128], hT[:, kc, blk(tb)], kc == 0, kc == 7,
                               [bw, bh[tb]], [bpt])
                        if which == 3:
                            o_ = [(0, 64, qTm[0:64, 2 * j, blk(tb)]), (64, 128, qTm[64:128, 2 * j + 1, blk(tb)])]
                        else:
                            o_ = dst[:, j, blk(tb)]
                        fin = pnorm_start(pt, bpt, 128, 512, bdiag[:], 1.0 / 64, gain[:, i:i + 1], o_,
                                          [bdst[tb]], sqt2[kk % 2], bsqt2[kk % 2])
                        kk += 1
                        if pend is not None:
                            pend()
                        pend = fin
                pend()
                if which == 4:
                    for tt in range(4):
                        pt, bpt = nps()
                        for kc in range(8):
                            MM(pt[:], hT[:, kc, tt * 128:(tt + 1) * 128], w[:, kc, :], kc == 0, kc == 7,
                               [bw, bh[0]], [bpt])
                        o, bo = ost[tt % 2], bost[tt % 2]
                        ACT(o[:], pt[:], AF.Square, [bpt], [bo])
                        P.op("dve", lambda h, o=o: h.tensor_reduce(out=sst[:, 0:8], in_=o[:].rearrange("p (h d) -> p h d", h=8),
                                                                   axis=mybir.AxisListType.X, op=ALU.add),
                             reads=[bo], writes=[bsst])
                        ACT(sst[:, 0:8], sst[:, 0:8], AF.Ln, [bsst, bconst], [bsst], bias=eps_sb[:, :], scale=1.0 / 64)
                        ACT(sst[:, 0:8], sst[:, 0:8], AF.Exp, [bsst], [bsst], scale=-0.5)
                        ov = o[:].rearrange("p (h d) -> p h d", h=8)
                        pv = pt[:].rearrange("p (h d) -> p h d", h=8)
                        for hh in range(8):
                            STT(ov[:, hh, :], pv[:, hh, :], sst[:, hh:hh + 1], gktok_sb[:, i, :], ALU.mult, ALU.mult,
                                [bpt, bsst, bsm], [bo])
                        STORE(nak_out[i, tt * 128:(tt + 1) * 128, :], o[:], bo)
            w, bw = wtile([(0, 512, W[:, :, 2560:3072])])
            qv = qTm.rearrange("p (j two) t -> p j two t", two=2)
            for tb_ in range(3):
                P.op("pool", lambda h, tb_=tb_: h.memset(qv[64:128, :, 0, blk(tb_)], 0.0), writes=[bq[tb_]])
                P.op("pool", lambda h, tb_=tb_: h.memset(qv[0:64, :, 1, blk(tb_)], 0.0), writes=[bq[tb_]])
            for tt in range(12):
                pt, bpt = nps()
                for kc in range(8):
                    MM(pt[:], hT[:, kc, tt * 128:(tt + 1) * 128], w[:, kc, :], kc == 0, kc == 7,
                       [bw, bh[tt // 4]], [bpt])
                COPY(V[:, tt, :], pt[:], [bpt], [bV[tt // 4]], eng="act")
                if tt < 4:
                    o, bo = ost[tt % 2], bost[tt % 2]
                    COPY(o[:], pt[:], [bpt], [bo])
                    STORE(nav_out[i, tt * 128:(tt + 1) * 128, :], o[:], bo)

            handover(phA, phC)
            batt = bh
            units = []
            for s_ in range(2):
                for hh in range(8):
                    j, b0 = hh // 2, (hh % 2) * 64
                    tiles = []
                    for c2 in range(2):
                        k0 = s_ * 256 + c2 * 128
                        tiles.append((kT[:, j, k0:k0 + 128], qTm[:, hh, s_ * 256:(s_ + 1) * 256],
                                      [bk[0], bq[0]], V[:, s_ * 2 + c2, j * 128:(j + 1) * 128], [bV[0]], None, None))
                    units.append(dict(tiles=tiles, b0=b0, out=attT[b0:b0 + 64, j, s_ * 256:(s_ + 1) * 256],
                                      outbuf=batt[0], ncq=256))
            chunks = [[0, 1, 2, 3], [0, 1, 2, 3, 4, 5], [2, 3, 4, 5, 6, 7], [4, 5, 6, 7]]
            for hh in range(8):
                j, b0 = hh // 2, (hh % 2) * 64
                g = hh % 2

                def pre(hh=hh, g=g):
                    LOAD(stg[g][:], rpbx[i][:, hh * 896:(hh + 1) * 896], bstg[g])
                    ACT(stg[g][:], stg[g][:], AF.Exp, [bstg[g]], [bstg[g]])
                    TT(Gf[g][:], stg[g][:], MFs[:], ALU.mult, [bstg[g], bM], [bG[g]])
                    TT(Gi[g][:], stg[g][:], MIs[:], ALU.mult, [bstg[g], bM], [bG[g]])
                for qh in range(2):
                    tbq = 1 + qh
                    q0 = 512 + qh * 512
                    tiles = []
                    for c in ([0, 1, 2, 3, 4, 5] if qh == 0 else [2, 3, 4, 5, 6, 7]):
                        k0 = 512 + c * 128
                        if qh == 0:
                            subs = [(0, 256, Gf[g][:, (6 - 2 * c) * 64:(10 - 2 * c) * 64] if c <= 3 else None),
                                    (256, 256, Gi[g][:, (10 - 2 * c) * 64:(14 - 2 * c) * 64])]
                        else:
                            subs = [(0, 256, Gi[g][:, (14 - 2 * c) * 64:(18 - 2 * c) * 64]),
                                    (256, 256, Gf[g][:, (18 - 2 * c) * 64:(22 - 2 * c) * 64] if c >= 4 else None)]
                        tiles.append((kT[:, j, k0:k0 + 128], qTm[:, hh, q0:q0 + 512],
                                      [bk[1 + c // 4], bq[tbq]], V[:, 4 + c, j * 128:(j + 1) * 128], [bV[1 + c // 4]],
                                      subs, [bG[g]]))
                    for c2 in range(2):
                        tiles.append((kcT[:, j, c2 * 128:(c2 + 1) * 128], qTm[:, hh, q0:q0 + 512],
                                      [bkc, bq[tbq]], V[:, 12 + c2, j * 128:(j + 1) * 128], [bVc], None, None))
                    units.append(dict(tiles=tiles, b0=b0, out=attT[b0:b0 + 64, j, q0:q0 + 512], outbuf=batt[tbq],
                                      ncq=512))
                    if qh == 0:
                        units[max(0, len(units) - 3)]["pre"] = pre
            pools["rr"] = [0, 1, 2, 5, 6] + x7
            for st_ in merged(unit_steps(units, PT, bPT), extra_steps):
                st_()
            pools["rr"] = [0, 1, 2, 3, 4, 5, 6] + x7
            handover(phC, [bsq] + btmp)
            out_proj(l, ewout[i], aT, baT, attT, batt, sq, bsq, tmp, btmp)
            del rstd[2:]
            del brstd[2:]
            return allb

        def odd_layer(l, prev_bufs):
            i = l // 2
            x7 = [7]
            pools["rr"] = [0, 1, 2, 3, 4, 5, 6] + x7
            ar = Arena()
            poolT = ar.take((128, 4, T), BF16)
            qln = ar.take((128, 3, T), BF16)
            ckn = ar.take((128, 2, 1792), BF16)
            kpe = ar.take((96, 1792))
            sqK = ar.take((96, 1792), BF16)
            Vall = ar.take((128, 14, 512), BF16)
            wkv = ar.take((128, 2, 1024), BF16)
            RC = ar.take((96, 1024))
            RS = ar.take((96, 1024))
            RM = ar.take((96, 96), BF16)
            sqt = ar.take((128, 3, 512), BF16)
            sst = ar.take((128, 8))
            u0 = ar.off
            sq = ar.take((128, 8, 512), BF16)
            tmp = ar.take((128, 2, 512))
            ar.off = u0
            utok = ar.take((128, 12, 512), BF16)
            diffT = ar.take((128, 4, T), BF16)
            PAs = ar.take((128, 36, 128), BF16)
            pw = ar.take((128, 4, 128), BF16)
            ost = [ar.take((128, 288)) for _ in range(2)]
            w2s = ar.take((128, 8, 160), BF16)
            ar.off = u0
            PT = [ar.take((128, 5120), BF16) for _ in range(2)]
            QT = [ar.take((96, T), BF16) for _ in range(2)]
            KT = [ar.take((96, 1792), BF16) for _ in range(2)]
            rt = [ar.take((96, 512)) for _ in range(2)]
            wq = ar.take((128, 3, 768), BF16)
            attT = hT[:, 0:4, :]
            ones96 = ones[0:96, 0:96]

            bsq, btmp = P.buf(), P.bufs(2)
            bpoolT, bqln, bckn = P.bufs(3, "poolT"), P.bufs(3, "qln"), P.bufs(4, "ckn")
            bkpe, bsqK, bVall = P.bufs(4, "kpe"), P.bufs(4, "sqK"), P.bufs(4, "Vall")
            bw_ = P.buf("wsm")
            bwB, bwC, bwR, bw2s = P.buf("wB"), P.buf("wC"), P.buf("wR"), P.buf("w2s")
            bsqt3, bQT, bKT, brt = P.bufs(3), P.bufs(2), P.bufs(2), P.bufs(2)
            bost, bsst = P.bufs(2), P.buf()
            butok, bdiff, bPT = P.bufs(3, "utok"), P.bufs(3, "diff"), [P.bufs(10, "PTa"), P.bufs(10, "PTb")]
            phA = [bsq] + btmp
            phB = butok + bdiff + [bwB, bw2s] + bost
            phC = bPT[0] + bPT[1] + bQT + bKT + brt + [bwC]
            allb = (phA + bpoolT + bqln + bckn + bkpe + bsqK + bVall + [bw_, bwR] + bsqt3 + [bsst] + phB + phC)
            handover(prev_bufs, [b for b in allb if b not in phB and b not in phC])

            W = kview(owin[i])
            w0, bw0 = wtile([(0, 512, W[:, :, 0:512])])
            w1_, bw1 = wtile([(0, 512, W[:, :, 512:1024])])
            LOAD(wkv[:], kview(wkvb[i]), bw_, q="pool")
            LOAD(RM[:], RMd, bw_, q="pool")
            LOAD(RC[:], RCd, bwR)
            LOAD(RS[:], RSd, bwR)
            LOAD(ckn[:, :, 0:256], kview(ckvT[i]), bckn[0], q="pool")
            LOAD(kpe[64:96, 0:256], kpeT[i], bkpe[0])

            W = kview(owin[i])
            handover(phA, phB)
            LOAD(w2s[:], W[:, :, 1024:1184], bw2s, q="pool")
            LOAD(pw[:], poolw[i].rearrange("p (g d) -> p g d", g=4), bwB, q="pool")
            for s3 in range(3):
                LOAD(PAs[:, s3 * 12:(s3 + 1) * 12, :], PAd[:, s3 * 1536:(s3 + 1) * 1536].rearrange("p (a t) -> p a t", a=12),
                     bwB, q="pool")

            for tt in range(12):
                pt, bpt = nps()
                for kc in range(8):
                    MM(pt[:], hT[:, kc, tt * 128:(tt + 1) * 128], w0[:, kc, :], kc == 0, kc == 7, [bw0, bh[tt // 4]], [bpt])
                COPY(utok[:, tt, :], pt[:], [bpt], [butok[tt // 4]], eng=("act" if tt % 2 else "dve"))
            w2_, bw2 = w2s, bw2s
            for tb in range(3):
                pts = []
                for c in range(3):
                    pt, bpt = nps()
                    for kc in range(8):
                        MM(pt[:], w1_[:, kc, c * 128:(c + 1) * 128], hT[:, kc, blk(tb)], kc == 0, kc == 7, [bw1, bh[tb]], [bpt])
                    ACT(sqt[:, c, :], pt[:], AF.Square, [bpt], [bsqt3[c]])
                    pts.append((pt, bpt))
                p2, bp2 = nps()
                for c in range(3):
                    MM(p2[:], ones[:], sqt[:, c, :], c == 0, c == 2, [bsqt3[c], bconst], [bp2])
                r, br = rstd_from(p2[:], bp2, 128, 512, 1.0 / 384)
                for c in range(3):
                    STT(qln[:, c, blk(tb)], pts[c][0][:], qag_sb[:, i, c:c + 1], r, ALU.mult, ALU.mult,
                        [pts[c][1], br, bsm], [bqln[tb]])
                pts = []
                for c in range(2):
                    pt, bpt = nps()
                    for kc in range(8):
                        lw_ = w1_[:, kc, 384:512] if c == 0 else w2_[:, kc, 0:128]
                        MM(pt[:], lw_, hT[:, kc, blk(tb)], kc == 0, kc == 7, [bw1, bw2, bh[tb]], [bpt])
                    ACT(sqt[:, c, :], pt[:], AF.Square, [bpt], [bsqt3[c]])
                    pts.append((pt, bpt))
                p2, bp2 = nps()
                for c in range(2):
                    MM(p2[:], ones[:], sqt[:, c, :], c == 0, c == 1, [bsqt3[c], bconst], [bp2])
                r, br = rstd_from(p2[:], bp2, 128, 512, 1.0 / 256)
                for c in range(2):
                    STT(ckn[:, c, 256 + tb * 512:256 + (tb + 1) * 512], pts[c][0][:], kvag_sb[:, i, c:c + 1], r,
                        ALU.mult, ALU.mult, [pts[c][1], br, bsm], [bckn[1 + tb]])
                pt, bpt = nps()
                for kc in range(8):
                    MM(pt[64:96, :], w2_[:, kc, 128:160], hT[:, kc, blk(tb)], kc == 0, kc == 7, [bw2, bh[tb]], [bpt])
                COPY(kpe[64:96, 256 + tb * 512:256 + (tb + 1) * 512], pt[64:96, :], [bpt], [bkpe[1 + tb]])
            for tt in range(4):
                pt, bpt = nps()
                for kc in range(8):
                    MM(pt[:, 0:128], hT[:, kc, tt * 128:(tt + 1) * 128], w1_[:, kc, 384:512], kc == 0, kc == 7,
                       [bw1, bh[0]], [bpt])
                for kc in range(8):
                    MM(pt[:, 128:288], hT[:, kc, tt * 128:(tt + 1) * 128], w2_[:, kc, 0:160], kc == 0, kc == 7,
                       [bw2, bh[0]], [bpt])
                o, bo = ost[tt % 2], bost[tt % 2]
                ACT(o[:, 0:256], pt[:, 0:256], AF.Square, [bpt], [bo])
                P.op("dve", lambda h, o=o: h.tensor_reduce(out=sst[:, 0:1], in_=o[:, 0:256], axis=mybir.AxisListType.X, op=ALU.add),
                     reads=[bo], writes=[bsst])
                ACT(sst[:, 0:1], sst[:, 0:1], AF.Ln, [bsst, bconst], [bsst], bias=eps_sb[:, :], scale=1.0 / 256)
                ACT(sst[:, 0:1], sst[:, 0:1], AF.Exp, [bsst], [bsst], scale=-0.5)
                STT(o[:, 0:256], pt[:, 0:256], sst[:, 0:1], kvg_sb[:, i, :], ALU.mult, ALU.mult, [bpt, bsst, bsm], [bo])
                COPY(o[:, 256:288], pt[:, 256:288], [bpt], [bo])
                STORE(ckv_out[i, tt * 128:(tt + 1) * 128, :], o[:, 0:256], bo)
                STORE(kpe_out[i, tt * 128:(tt + 1) * 128, :], o[:, 256:288], bo)

            s_ = wt_i[0] % NW
            wt_i[0] += 1
            wq = WT[s_][:].rearrange("p a b -> p (a b)")[:, 0:2304].rearrange("p (k n) -> p k n", k=3)
            bwC = bWT[s_]
            P.dma("pool", lambda h: h.dma_start(out=wq, in_=kview(wqb[i])), writes=[bwC], sbuf=bwC)
            seq_tiles = [(0, 2), (2, 2), (4, 8)]
            case_of = {}
            for (t0, n_) in seq_tiles:
                for k_ in range(n_):
                    case_of[t0 + k_] = (0 if k_ == 0 else (2 if k_ == n_ - 1 else 1), t0, t0 + n_)
            for g in range(4):
                for tb in range(3):
                    pt, bpt = nps()
                    for q_ in range(4):
                        tt = tb * 4 + q_
                        case, lo_, hi_ = case_of[tt]
                        rels = [r_ for r_ in (-1, 0, 1) if lo_ <= tt + r_ < hi_]
                        for n_, r_ in enumerate(rels):
                            MM(pt[:, q_ * 128:(q_ + 1) * 128], utok[:, tt + r_, g * 128:(g + 1) * 128],
                               PAs[:, case * 12 + g * 3 + (r_ + 1), :], n_ == 0, n_ == len(rels) - 1,
                               [butok[(tt + r_) // 4], bwB], [bpt])
                    COPY(diffT[:, g, blk(tb)], pt[:], [bpt], [bdiff[tb]], eng=("act" if (g + tb) % 2 else "dve"))
            for g in range(4):
                for tb in range(3):
                    pt, bpt = nps()
                    MM(pt[:], pw[:, g, :], diffT[:, g, blk(tb)], True, True, [bwB, bdiff[tb]], [bpt])
                    TS(poolT[:, g, blk(tb)], pt[:], pscale_sb[:, i, g:g + 1], None, ALU.mult, None, [bpt, bsm], [bpoolT[tb]])

            handover(phB, phC)
            wv = wkv[:].rearrange("p k (h e) -> p k h e", h=8)[:, :, :, 64:128]
            vsteps = []
            for vt in range(14):
                def vs_(vt=vt):
                    pt, bpt = nps()
                    c0 = vt * 128
                    for kc in range(2):
                        MM(pt[:].rearrange("p (h e) -> p h e", h=8), ckn[:, kc, c0:c0 + 128], wv[:, kc, :, :], kc == 0, kc == 1,
                           [bckn[(c0 + 256) // 512 if vt >= 2 else 0], bw_], [bpt])
                    COPY(Vall[:, vt, :], pt[:], [bpt], [bVall[(c0 + 256) // 512 if vt >= 2 else 0]],
                         eng=("act" if vt % 2 else "dve"))
                vsteps.append(vs_)
            for kb in range(4):
                c0, c1 = (0, 256) if kb == 0 else (256 + (kb - 1) * 512, 256 + kb * 512)
                ACT(sqK[64:96, c0:c1], kpe[64:96, c0:c1], AF.Square, [bkpe[kb]], [bsqK[kb]])

            batt = bh
            nsq = [0]

            def prod_steps(hh):
                Qh, bQh = QT[hh % 2], bQT[hh % 2]
                Kh, bKh = KT[hh % 2], bKT[hh % 2]
                Qs, Ks = [], []
                for tb in range(3):
                    box = {}

                    def s1(tb=tb, box=box):
                        pt, bpt = hps()
                        for kc in range(3):
                            MM(pt[0:96, :], wq[:, kc, hh * 96:(hh + 1) * 96], qln[:, kc, blk(tb)], kc == 0, kc == 2,
                               [bwC, bqln[tb]], [bpt])
                        k_ = nsq[0] % 3
                        nsq[0] += 1
                        box["fin"] = pnorm_start(pt, bpt, 96, 512, ones96, 1.0 / 96, gqm_sb[:, i:i + 1], Qh[:, blk(tb)],
                                                 [bQh], sqt[:, k_, :], bsqt3[k_])
                    rp = None
                    if tb >= 1:
                        rp = (lambda tb=tb: rope(Qh, bQh, tb * 512, (tb - 1) * 512, RM, RC, RS, rt, brt, [bw_, bwR]))
                    Qs.append((s1, (lambda box=box: box["fin"]()), rp))
                for kb in range(4):
                    c0, c1 = (0, 256) if kb == 0 else (256 + (kb - 1) * 512, 256 + kb * 512)
                    n_ = c1 - c0
                    box = {}

                    def k1(kb=kb, c0=c0, c1=c1, n_=n_, box=box):
                        pt, bpt = hps()
                        for kc in range(2):
                            MM(pt[0:64, 0:n_], wkv[:, kc, hh * 128:hh * 128 + 64], ckn[:, kc, c0:c1], kc == 0, kc == 1,
                               [bw_, bckn[kb]], [bpt])
                        ACT(sqK[0:64, c0:c1], pt[0:64, 0:n_], AF.Square, [bpt], [bsqK[kb]])
                        box["pt"] = (pt, bpt)

                    def k2(kb=kb, c0=c0, c1=c1, n_=n_, box=box):
                        pt, bpt = box["pt"]
                        p2, bp2 = nps()
                        MM(p2[0:96, 0:n_], ones96, sqK[:, c0:c1], True, True, [bsqK[kb], bconst], [bp2])
                        r, br = rstd_from(p2[0:96, 0:n_], bp2, 96, n_, 1.0 / 96)
                        STT(Kh[0:64, c0:c1], pt[0:64, 0:n_], gkm_sb[0:64, i:i + 1], r[0:64, :], ALU.mult, ALU.mult,
                            [bpt, br, bsm], [bKh])
                        STT(Kh[64:96, c0:c1], kpe[64:96, c0:c1], gkm_sb[64:96, i:i + 1], r[64:96, :], ALU.mult, ALU.mult,
                            [bkpe[kb], br, bsm], [bKh])
                    rp = None
                    if kb >= 2:
                        rp = (lambda kb=kb, c0=c0: rope(Kh, bKh, c0, (kb - 2) * 512, RM, RC, RS, rt, brt, [bw_, bwR]))
                    Ks.append((k1, k2, rp))
                (s1_0, fq_0, _), (s1_1, fq_1, rq_1), (s1_2, fq_2, rq_2) = Qs
                (k1_0, k2_0, _), (k1_1, k2_1, _), (k1_2, k2_2, rk_2), (k1_3, k2_3, rk_3) = Ks
                return [s1_0, k1_0, fq_0, k2_0,
                        s1_1, k1_1, fq_1, k2_1,
                        s1_2, k1_2, rq_1, fq_2, k2_2,
                        k1_3, rq_2, k2_3, rk_2, rk_3]

            def att_units(hh):
                j, b0 = hh // 2, (hh % 2) * 64
                Qh, bQh = QT[hh % 2], bQT[hh % 2]
                Kh, bKh = KT[hh % 2], bKT[hh % 2]
                units = []
                for s_ in range(2):
                    tiles = []
                    for c2 in range(2):
                        k0 = 256 + s_ * 256 + c2 * 128
                        tiles.append((Kh[:, k0:k0 + 128], Qh[:, s_ * 256:(s_ + 1) * 256], [bKh, bQh],
                                      Vall[:, 2 + s_ * 2 + c2, j * 128:(j + 1) * 128], [bVall[1]], None, None))
                    units.append(dict(tiles=tiles, b0=b0, out=attT[b0:b0 + 64, j, s_ * 256:(s_ + 1) * 256],
                                      outbuf=batt[0], ncq=256))
                for qh in range(2):
                    q0 = 512 + qh * 512
                    tiles = []
                    for c in range(10):
                        k0 = c * 128 if c < 2 else 768 + (c - 2) * 128
                        vt = c if c < 2 else 6 + (c - 2)
                        vb_ = bVall[0] if c < 2 else bVall[2 + (c - 2) // 4]
                        tiles.append((Kh[:, k0:k0 + 128], Qh[:, q0:q0 + 512], [bKh, bQh],
                                      Vall[:, vt, j * 128:(j + 1) * 128], [vb_], None, None))
                    units.append(dict(tiles=tiles, b0=b0, out=attT[b0:b0 + 64, j, q0:q0 + 512], outbuf=batt[1 + qh],
                                      ncq=512))
                return units

            pools["rr"] = [0, 1, 2, 7]
            for st_ in merged(vsteps, prod_steps(0)):
                st_()
            for hh in range(8):
                A = unit_steps(att_units(hh), PT, bPT)
                B = prod_steps(hh + 1) if hh + 1 < 8 else []
                for st_ in merged(A, B):
                    st_()
            pools["rr"] = [0, 1, 2, 3, 4, 5, 6] + x7
            handover(phC, [bsq] + btmp)
            out_proj(l, owout[i], poolT, bpoolT, attT, batt, sq, bsq, tmp, btmp)
            return allb

        def rope(X, bX, c0, p0, RM, RC, RS, rt, brt, bw_):
            pt, bpt = nps()
            MM(pt[0:96, :], RM[:], X[:, c0:c0 + 512], True, True, [bX, bw_[0]], [bpt])
            TT(rt[0][64:96, :], X[64:96, c0:c0 + 512], RC[64:96, p0:p0 + 512], ALU.mult, [bX] + bw_[1:], [brt[0]])
            TT(rt[1][64:96, :], pt[64:96, :], RS[64:96, p0:p0 + 512], ALU.mult, [bpt] + bw_[1:], [brt[1]])
            TT(X[64:96, c0:c0 + 512], rt[0][64:96, :], rt[1][64:96, :], ALU.add, [brt[0], brt[1]], [bX])


        ada0A, ada0B = ada_steps(0, split=True)
        for st_ in ada0A:
            st_()
        for tb in (1, 2):
            P.dma("sp", lambda h, tb=tb: h.dma_start(out=yT[:, :, blk(tb)], in_=kview(xT)[:, :, blk(tb)]),
                  reads=[bWT[0], bWT[1]], writes=[by[tb]], sbuf=by[tb])
        prev = []
        for l in range(depth):
            if l % 2 == 0:
                prev = even_layer(l, prev, ada0B if l == 0 else [])
            else:
                prev = odd_layer(l, prev)
            prev = mlp(l, prev, l + 1 < depth)

        P.finish()
        P.emit()
    return nc


_NC_CACHE = {}


def _prep_shared(inp):
    f = lambda k: np.ascontiguousarray(np.asarray(inp[k], dtype=np.float32))
    sh = {}
    for k in ("ada_w", "mlp_w1", "mlp_w2", "even_w_in", "even_w_out", "odd_w_in", "odd_w_out", "w_q_b", "w_kv_b"):
        sh[k] = f(k)
    sh["ada_b"] = np.ascontiguousarray(f("ada_b").reshape(4, 48, 128).transpose(2, 0, 1)).reshape(128, 192)
    sh["n1g"] = np.ascontiguousarray(f("norm1_g").reshape(4, 8, 128).transpose(2, 0, 1)).reshape(128, 32)
    sh["n2g"] = np.ascontiguousarray(f("norm2_g").reshape(4, 8, 128).transpose(2, 0, 1)).reshape(128, 32)
    sh["convw"] = np.ascontiguousarray(f("even_conv_w").reshape(2, 3, 4, 128).transpose(3, 0, 1, 2)).reshape(128, 24)
    sh["gqna"] = np.ascontiguousarray(np.tile(f("na_q_norm").T, (2, 1)))
    sh["gkna"] = np.ascontiguousarray(np.tile(f("na_k_norm").T, (2, 1)))
    sh["gktok"] = np.ascontiguousarray(np.broadcast_to(f("na_k_norm")[None], (128, 2, 64))).reshape(128, 128)
    sh["rpbx"] = _rpb_expand(f("na_rpb"))
    sh["poolw"] = np.ascontiguousarray(f("pool_w").transpose(0, 2, 1, 3)).reshape(2, 128, 512)
    sh["pscale"] = np.ascontiguousarray(f("pool_scale").reshape(2, 4, 128).transpose(2, 0, 1)).reshape(128, 8)
    sh["qag"] = np.ascontiguousarray(f("q_a_norm").reshape(2, 3, 128).transpose(2, 0, 1)).reshape(128, 6)
    sh["kvag"] = np.ascontiguousarray(f("kv_a_norm").reshape(2, 2, 128).transpose(2, 0, 1)).reshape(128, 4)
    sh["kvgrep"] = np.ascontiguousarray(np.broadcast_to(f("kv_a_norm")[None], (128, 2, 256))).reshape(128, 512)
    sh["gqm"] = np.ascontiguousarray(f("mla_q_norm").T)
    sh["gkm"] = np.ascontiguousarray(f("mla_k_norm").T)
    sh.update(_const_tables())
    return sh


def _prep_core(inp, c):
    f = lambda k: np.asarray(inp[k], dtype=np.float32)
    m = {}
    xp = f("x_prompt")[2 * c:2 * c + 2].reshape(512, D)
    xs = f("x_sample")[c]
    m["xT"] = np.ascontiguousarray(np.concatenate([xp, xs], 0).T)
    cc = np.stack([f("c_ctx"), f("c")[c]], 1)
    m["cT"] = np.ascontiguousarray(cc.reshape(8, 128, 2).transpose(1, 0, 2)).reshape(128, 16)
    ck = f("cache_na_k")[c]
    kk = ck.transpose(0, 2, 3, 1).reshape(2, 4, 128, 256).transpose(0, 2, 1, 3)
    m["nakT"] = np.ascontiguousarray(kk).reshape(2, 128, 1024)
    m["nav"] = np.ascontiguousarray(f("cache_na_v")[c].reshape(2, 256, 512))
    m["ckvT"] = np.ascontiguousarray(f("cache_mla_ckv")[c].transpose(0, 2, 1))
    m["kpeT"] = np.ascontiguousarray(f("cache_mla_kpe")[c].transpose(0, 2, 1))
    return m


DEPTH = 4


def kernel(**inputs):
    depth = DEPTH
    if depth not in _NC_CACHE:
        _NC_CACHE[depth] = build(depth)
    nc = _NC_CACHE[depth]
    sh = _prep_shared(inputs)
    in_maps = []
    for c in range(NCORES):
        m = dict(sh)
        m.update(_prep_core(inputs, c))
        in_maps.append(m)
    res = run_bass_kernel_spmd(nc, in_maps, core_ids=list(range(NCORES)))
    R = res.results
    yp = np.zeros((16, 256, D), np.float32)
    ys = np.zeros((8, 1024, D), np.float32)
    nak = np.zeros((16, 2, 256, 8, 64), np.float32)
    nav = np.zeros((16, 2, 256, 8, 64), np.float32)
    ckv = np.zeros((16, 2, 256, 256), np.float32)
    kpe = np.zeros((16, 2, 256, 32), np.float32)
    for c in range(NCORES):
        y = np.asarray(R[c]["yT_out"]).T
        yp[2 * c:2 * c + 2] = y[0:512].reshape(2, 256, D)
        ys[c] = y[512:]
        for s in range(2):
            nak[2 * c + s] = np.asarray(R[c]["nak_out"])[:, s * 256:(s + 1) * 256].reshape(2, 256, 8, 64)
            nav[2 * c + s] = np.asarray(R[c]["nav_out"])[:, s * 256:(s + 1) * 256].reshape(2, 256, 8, 64)
            ckv[2 * c + s] = np.asarray(R[c]["ckv_out"])[:, s * 256:(s + 1) * 256]
            kpe[2 * c + s] = np.asarray(R[c]["kpe_out"])[:, s * 256:(s + 1) * 256]
    return (yp, ys, nak, nav, ckv, kpe)
```
